# Optimizing a Trainium2 kernel written in Bass

```python
import jax, jax.numpy as jnp
from jax import lax
import numpy as np

D_MODEL = 2048
BATCH = 8
SEQ = 2048
DEPTH = 2
DEC_BATCH = 8
DEC_SEQ = 16
PAST_LEN = 2048

CHUNK = 64
EPS = 1e-6
A_WIDTH = 1024
A_GROUPS = 4
A_GROUP_DIM = A_WIDTH // A_GROUPS
A_CHUNK = 128
B_HEADS = 16
B_Q_RANK = 512
B_KV_RANK = 512
B_NOPE = 128
B_ROPE = 64
B_VDIM = 128
B_QK = B_NOPE + B_ROPE
B_WIDTH = B_HEADS * B_VDIM
ATTN_SCALE = B_QK ** -0.5
ROPE_BASE = 10000.0
Q_BLOCK = 128
C_WIDTH = 1024
C_CONV = 3
N_BRANCH = 3

IN_SIZES = (A_WIDTH, A_WIDTH, A_WIDTH,
            B_Q_RANK, B_KV_RANK, B_ROPE, B_WIDTH,
            C_WIDTH, C_WIDTH, C_WIDTH, C_WIDTH,
            N_BRANCH * D_MODEL)
IN_SPLITS = tuple(sum(IN_SIZES[:i + 1]) for i in range(len(IN_SIZES) - 1))
N_IN = sum(IN_SIZES)

kernel_name = 'hybrid_chunk_streaming_encoder_step'


def rms_norm(x, g):
    xf = x.astype(jnp.float32)
    y = xf * lax.rsqrt(jnp.mean(xf * xf, axis=-1, keepdims=True) + EPS)
    return (y * g.astype(jnp.float32)).astype(x.dtype)


def rope(x, pos):
    half = x.shape[-1] // 2
    inv = ROPE_BASE ** (-jnp.arange(half, dtype=jnp.float32) / half)
    ang = pos[:, None] * inv[None, :]
    ang = ang.reshape(ang.shape[:1] + (1,) * (x.ndim - 3) + (half,))
    c, s = jnp.cos(ang), jnp.sin(ang)
    xf = x.astype(jnp.float32)
    x1, x2 = xf[..., :half], xf[..., half:]
    return jnp.concatenate([x1 * c - x2 * s, x2 * c + x1 * s], axis=-1).astype(x.dtype)


def spatial_gate_prompt(u, vn, ws_m, bias):
    bn, s, _ = vn.shape
    n = s // A_CHUNK
    vg = vn.reshape(bn, n, A_CHUNK, A_GROUPS, A_GROUP_DIM)
    mixed = jnp.einsum('gts,bnsgd->bntgd', ws_m, vg) + bias.T[:, :, None]
    return u * mixed.reshape(bn, s, A_WIDTH)


def spatial_gate_sample(u, vn, ws_m, bias):
    bn, t, _ = vn.shape
    vg = vn.reshape(bn, t, A_GROUPS, A_GROUP_DIM)
    mixed = jnp.einsum('gts,bsgd->btgd', ws_m[:, :t, :t], vg) + bias[:, :t].T[:, :, None]
    return u * mixed.reshape(bn, t, A_WIDTH)


def short_conv(hc, state, w):
    t = hc.shape[1]
    padded = jnp.concatenate([state, hc], axis=1)
    y = sum(w[k] * padded[:, k:k + t] for k in range(C_CONV))
    return y, padded[:, t:]


def mla_queries(b_cq, p, pos):
    cq = rms_norm(b_cq, p['b_q_gain'])
    q = (cq @ p['b_w_qb']).reshape(cq.shape[:2] + (B_HEADS, B_QK))
    q_nope = rms_norm(q[..., :B_NOPE], p['b_qn_gain'])
    q_pe = rope(rms_norm(q[..., B_NOPE:], p['b_qr_gain']), pos)
    return jnp.concatenate([q_nope, q_pe], axis=-1)


def mla_keys_values(ckv_all, kpe_all, p):
    kv = (ckv_all @ p['b_w_kvb']).reshape(ckv_all.shape[:2] + (B_HEADS, B_NOPE + B_VDIM))
    k_nope = rms_norm(kv[..., :B_NOPE], p['b_kn_gain'])
    k_pe = jnp.broadcast_to(kpe_all[:, :, None, :], k_nope.shape[:3] + (B_ROPE,))
    return jnp.concatenate([k_nope, k_pe], axis=-1), kv[..., B_NOPE:]


def attend_prompt(q, k, v):
    bn, s, h, dq = q.shape
    nb = s // Q_BLOCK
    qb = q.reshape(bn, nb, Q_BLOCK, h, dq).swapaxes(0, 1)
    kchunk = jnp.arange(s) // CHUNK

    def one_block(args):
        qblk, i = args
        qchunk = (i * Q_BLOCK + jnp.arange(Q_BLOCK)) // CHUNK
        sc = jnp.einsum('bqhd,bkhd->bhqk', qblk, k, preferred_element_type=jnp.float32) * ATTN_SCALE
        sc = jnp.where((kchunk[None, :] <= qchunk[:, None])[None, None], sc, -jnp.inf)
        pr = jax.nn.softmax(sc, axis=-1).astype(v.dtype)
        return jnp.einsum('bhqk,bkhd->bqhd', pr, v)

    o = lax.map(one_block, (qb, jnp.arange(nb)))
    return o.swapaxes(0, 1).reshape(bn, s, h * B_VDIM)


def attend_sample(q, k, v):
    bn, t = q.shape[:2]
    sc = jnp.einsum('bqhd,bkhd->bhqk', q, k, preferred_element_type=jnp.float32) * ATTN_SCALE
    pr = jax.nn.softmax(sc, axis=-1).astype(v.dtype)
    return jnp.einsum('bhqk,bkhd->bqhd', pr, v).reshape(bn, t, B_HEADS * B_VDIM)


def layer(x, pos, p, conv_state, past_ckv, past_kpe, prompt):
    xn = rms_norm(x, p['norm'])
    proj = xn @ p['w_in']
    (a_u, a_v, a_z, b_cq, b_ckv, b_kpe, b_z,
     c_h, c_b, c_c, c_z, gates) = jnp.split(proj, IN_SPLITS, axis=-1)

    a_vn = rms_norm(a_v.reshape(a_v.shape[:2] + (A_GROUPS, A_GROUP_DIM)),
                    p['a_v_gain'].reshape(A_GROUPS, A_GROUP_DIM)).reshape(a_v.shape)
    ws_m = p['a_ws'] * jnp.tril(jnp.ones((A_CHUNK, A_CHUNK), p['a_ws'].dtype))
    if prompt:
        ya = spatial_gate_prompt(a_u, a_vn, ws_m, p['a_bias'])
    else:
        ya = spatial_gate_sample(a_u, a_vn, ws_m, p['a_bias'])
    ya = ya * jax.nn.silu(a_z)

    ckv_new = rms_norm(b_ckv, p['b_kv_gain'])
    kpe_new = rope(rms_norm(b_kpe, p['b_kr_gain']), pos)
    q = mla_queries(b_cq, p, pos)
    if prompt:
        ckv_all, kpe_all = ckv_new, kpe_new
    else:
        ckv_all = jnp.concatenate([past_ckv, ckv_new], axis=1)
        kpe_all = jnp.concatenate([past_kpe, kpe_new], axis=1)
    k, v = mla_keys_values(ckv_all, kpe_all, p)
    attn = attend_prompt(q, k, v) if prompt else attend_sample(q, k, v)
    yb = attn * jax.nn.silu(b_z)

    conv_y, conv_new = short_conv(c_c * c_h, conv_state, p['c_conv_w'])
    yc = c_b * conv_y * jax.nn.silu(c_z)

    g = jax.nn.sigmoid(gates.astype(jnp.float32)).astype(x.dtype)
    g = g.reshape(g.shape[:-1] + (N_BRANCH, D_MODEL))
    h = (g[..., 0, :] * (ya @ p['w_pa']) + g[..., 1, :] * (yb @ p['w_pb'])
         + g[..., 2, :] * (yc @ p['w_pc']))
    return x + h @ p['w_out'], ckv_new, kpe_new, conv_new, a_vn


def setup_inputs(seed: int = 0) -> dict:
    key = jax.random.key(seed)
    ks = jax.random.split(key, 24)

    def nrm(k, shape, scale):
        return jax.random.normal(k, shape, jnp.float32) * scale

    def gain(k, shape):
        return 1.0 + 0.05 * jax.random.normal(k, shape, jnp.float32)

    return {
        'x_prompt': nrm(ks[0], (BATCH, SEQ, D_MODEL), 1.0),
        'x_sample': nrm(ks[1], (DEC_BATCH, DEC_SEQ, D_MODEL), 1.0),
        'cache_ckv': nrm(ks[2], (DEPTH, DEC_BATCH, PAST_LEN, B_KV_RANK), 1.0),
        'cache_kpe': nrm(ks[3], (DEPTH, DEC_BATCH, PAST_LEN, B_ROPE), 1.0),
        'state_conv': nrm(ks[4], (DEPTH, DEC_BATCH, C_CONV - 1, C_WIDTH), 1.0),
        'norm_gain': gain(ks[5], (DEPTH, D_MODEL)),
        'w_in': nrm(ks[6], (DEPTH, D_MODEL, N_IN), D_MODEL ** -0.5),
        'a_v_gain': gain(ks[7], (DEPTH, A_WIDTH)),
        'a_ws': nrm(ks[8], (DEPTH, A_GROUPS, A_CHUNK, A_CHUNK), A_CHUNK ** -0.5),
        'a_bias': 1.0 + 0.1 * jax.random.normal(ks[9], (DEPTH, A_GROUPS, A_CHUNK), jnp.float32),
        'b_q_gain': gain(ks[10], (DEPTH, B_Q_RANK)),
        'b_w_qb': nrm(ks[11], (DEPTH, B_Q_RANK, B_HEADS * B_QK), B_Q_RANK ** -0.5),
        'b_kv_gain': gain(ks[12], (DEPTH, B_KV_RANK)),
        'b_kr_gain': gain(ks[13], (DEPTH, B_ROPE)),
        'b_w_kvb': nrm(ks[14], (DEPTH, B_KV_RANK, B_HEADS * (B_NOPE + B_VDIM)), B_KV_RANK ** -0.5),
        'b_qn_gain': gain(ks[15], (DEPTH, B_NOPE)),
        'b_qr_gain': gain(ks[16], (DEPTH, B_ROPE)),
        'b_kn_gain': gain(ks[17], (DEPTH, B_NOPE)),
        'c_conv_w': nrm(ks[18], (DEPTH, C_CONV, C_WIDTH), C_CONV ** -0.5),
        'w_pa': nrm(ks[19], (DEPTH, A_WIDTH, D_MODEL), A_WIDTH ** -0.5),
        'w_pb': nrm(ks[20], (DEPTH, B_WIDTH, D_MODEL), B_WIDTH ** -0.5),
        'w_pc': nrm(ks[21], (DEPTH, C_WIDTH, D_MODEL), C_WIDTH ** -0.5),
        'w_out': nrm(ks[22], (DEPTH, D_MODEL, D_MODEL), D_MODEL ** -0.5),
    }


def reference(x_prompt, x_sample, cache_ckv, cache_kpe, state_conv,
              norm_gain, w_in, a_v_gain, a_ws, a_bias,
              b_q_gain, b_w_qb, b_kv_gain, b_kr_gain, b_w_kvb,
              b_qn_gain, b_qr_gain, b_kn_gain, c_conv_w,
              w_pa, w_pb, w_pc, w_out):
    def params(l):
        return {'norm': norm_gain[l], 'w_in': w_in[l], 'a_v_gain': a_v_gain[l],
                'a_ws': a_ws[l], 'a_bias': a_bias[l], 'b_q_gain': b_q_gain[l],
                'b_w_qb': b_w_qb[l], 'b_kv_gain': b_kv_gain[l], 'b_kr_gain': b_kr_gain[l],
                'b_w_kvb': b_w_kvb[l], 'b_qn_gain': b_qn_gain[l], 'b_qr_gain': b_qr_gain[l],
                'b_kn_gain': b_kn_gain[l], 'c_conv_w': c_conv_w[l], 'w_pa': w_pa[l],
                'w_pb': w_pb[l], 'w_pc': w_pc[l], 'w_out': w_out[l]}

    hp = x_prompt
    pos_p = jnp.arange(x_prompt.shape[1], dtype=jnp.float32)
    conv0 = jnp.zeros((x_prompt.shape[0], C_CONV - 1, C_WIDTH), x_prompt.dtype)
    ckv_p, kpe_p, conv_p = [], [], []
    for l in range(DEPTH):
        hp, ckv, kpe, cst, _ = layer(hp, pos_p, params(l), conv0, None, None, True)
        ckv_p.append(ckv)
        kpe_p.append(kpe)
        conv_p.append(cst)

    hs = x_sample
    past_len = cache_ckv.shape[2]
    pos_s = past_len + jnp.arange(x_sample.shape[1], dtype=jnp.float32)
    ckv_s, kpe_s, conv_s, av_s = [], [], [], []
    for l in range(DEPTH):
        hs, ckv, kpe, cst, avn = layer(hs, pos_s, params(l), state_conv[l],
                                       cache_ckv[l], cache_kpe[l], False)
        ckv_s.append(ckv)
        kpe_s.append(kpe)
        conv_s.append(cst)
        av_s.append(avn)

    return (hp, hs,
            jnp.stack(ckv_p), jnp.stack(kpe_p), jnp.stack(conv_p),
            jnp.stack(ckv_s), jnp.stack(kpe_s), jnp.stack(conv_s), jnp.stack(av_s))
```

```python
import math
import numpy as np
import concourse.bass as bass
import concourse.mybir as mybir
from concourse.bass_utils import run_bass_kernel_spmd

F32 = mybir.dt.float32
BF16 = mybir.dt.bfloat16
AF = mybir.ActivationFunctionType
ALU = mybir.AluOpType

D = 2048
SEQ = 2048
DEPTH = 2
NIN = 16448
EPS = 1e-6
SCALE = 192 ** -0.5
C_AU, C_AV, C_AZ, C_CQ, C_CKV, C_KPE, C_BZ = 0, 1024, 2048, 3072, 3584, 4096, 4160
C_CH, C_CB, C_CC, C_CZ, C_G = 6208, 7232, 8256, 9280, 10304
NPOS = 17 * 128
WIN_BLOCKS = sorted([C_AU, C_AU + 512, C_AV, C_AV + 512, C_AZ, C_AZ + 512, C_CQ, C_CKV] + [C_BZ + b * 512 for b in range(4)]
                    + [c + b * 512 for c in (C_CH, C_CB, C_CC, C_CZ) for b in range(2)] + [C_G + j * 512 for j in range(12)])
WIN_IDX = {c: i for i, c in enumerate(WIN_BLOCKS)}
USE_WSC = True
LA = 2


class Sched:
    def __init__(self, nc):
        self.nc = nc
        self.E = {'pe': nc.tensor, 'act': nc.scalar, 'dve': nc.vector, 'pool': nc.gpsimd, 'sp': nc.sync}
        self.sem = {}
        self.cnt = {}
        self.waited = {e: {} for e in self.E}
        self.buf = {}
        self.dsem = {}
        for e in self.E:
            self.sem[e] = nc.alloc_semaphore('pg_' + e)
            self.cnt[e] = 0

    def _wait(self, E, reads, writes, extra=()):
        need = {}

        def add(st):
            if st is None:
                return
            k, v = st
            if need.get(k, 0) < v:
                need[k] = v
        for r in reads:
            b = self.buf.get(r)
            if b:
                add(b[0])
        for w in writes:
            b = self.buf.get(w)
            if b:
                add(b[0])
                for k, v in b[1].items():
                    add((k, v))
        for st in extra:
            add(st)
        for k, v in need.items():
            if k == 'pe' and E == 'pe':
                continue
            if self.waited[E].get(k, 0) < v:
                self.waited[E][k] = v
                self.E[E].wait_ge(self.sem[k], v)

    def _mark(self, st, reads, writes):
        k, v = st
        for r in reads:
            b = self.buf.setdefault(r, [None, {}])
            if b[1].get(k, 0) < v:
                b[1][k] = v
        for w in writes:
            self.buf[w] = [st, {}]

    def op(self, E, fn, r=(), w=(), inc=True):
        self._wait(E, r, w)
        ins = fn(self.E[E])
        if inc:
            self.cnt[E] += 1
            ins.then_inc(self.sem[E], 1)
            st = (E, self.cnt[E])
        else:
            st = (E, self.cnt[E] + 1)
        self._mark(st, r, w)

    def dma(self, Q, out, in_, key, r=(), w=(), nc_ok=False):
        if key not in self.dsem:
            name = 'd%d' % len(self.dsem)
            self.sem[name] = self.nc.alloc_semaphore(name)
            self.dsem[key] = [name, 0]
        ds = self.dsem[key]
        extra = [(ds[0], ds[1])] if ds[1] > 0 else []
        self._wait(Q, r, w, extra)
        ds[1] += 16
        if nc_ok:
            ins = self.E[Q].dma_start(out=out, in_=in_, allow_slow_non_contiguous=True)
        else:
            ins = self.E[Q].dma_start(out=out, in_=in_)
        ins.then_inc(self.sem[ds[0]], 16)
        self._mark((ds[0], ds[1]), r, w)

    def finish(self, Q):
        for key, ds in self.dsem.items():
            if ds[1] > 0 and self.waited[Q].get(ds[0], 0) < ds[1]:
                self.waited[Q][ds[0]] = ds[1]
                self.E[Q].wait_ge(self.sem[ds[0]], ds[1])


def build(nptiles=4, depth=DEPTH, sample=True):
    nc = bass.Bass("TRN2", target_bir_lowering=False)

    def din(name, shape):
        return nc.dram_tensor(name, list(shape), F32, kind="ExternalInput").ap()

    def dout(name, shape):
        return nc.dram_tensor(name, list(shape), F32, kind="ExternalOutput").ap()

    xp = din("xp", [SEQ, D]); xs = din("xs", [16, D])
    cckv = din("cckv", [2, 2048, 512]); ckpe = din("ckpe", [2, 2048, 64]); sconv = din("sconv", [2, 2, 1024])
    norm_gain = din("norm_gain", [2, D]); w_in_t = din("w_in_t", [2, len(WIN_BLOCKS), 128, 8192]); w_kpe = din("w_kpe", [2, D, 64])
    a_v_gain = din("a_v_gain", [2, 1024]); a_ws = din("a_ws", [2, 4, 128, 128]); a_bias = din("a_bias", [2, 1, 512])
    b_q_gain = din("b_q_gain", [2, 512]); w_qkv_t = din("w_qkv_t", [2, 4, 128, 8192])
    b_kv_gain = din("b_kv_gain", [2, 512]); b_kr_gain = din("b_kr_gain", [2, 64])
    b_qn_gain = din("b_qn_gain", [2, 128]); qr2 = din("qr2", [2, 2, 64]); b_kn_gain = din("b_kn_gain", [2, 128])
    c_conv_w = din("c_conv_w", [2, 3, 1024])
    w_pa = din("w_pa_t", [2, 4, 128, 4096]); w_pb = din("w_pb_t", [2, 4, 128, 8192]); w_pc = din("w_pc_t", [2, 4, 128, 4096])
    w_out = din("w_out_t", [2, 4, 128, 8192])
    rope_tm = din("rope_tm", [NPOS, 64]); rope_fm = din("rope_fm", [2, 64, NPOS])

    yp = dout("yp", [SEQ, D]); ys = dout("ys", [16, D])
    o_ckv_p = dout("o_ckv_p", [2, SEQ, 512]); o_kpe_p = dout("o_kpe_p", [2, SEQ, 64]); o_conv_p = dout("o_conv_p", [2, 2, 1024])
    o_ckv_s = dout("o_ckv_s", [2, 16, 512]); o_kpe_s = dout("o_kpe_s", [2, 16, 64]); o_conv_s = dout("o_conv_s", [2, 2, 1024])
    o_av_s = dout("o_av_s", [2, 16, 1024])

    Ksc = nc.dram_tensor("Ksc", [2, 2, 16, 128, 2048], BF16).ap()
    Vsc = nc.dram_tensor("Vsc", [2, 2, 4, 16, 128, 512], BF16).ap()
    Wsc = nc.dram_tensor("Wsc", [2, 56, 128, 8192], BF16).ap()

    S = Sched(nc)
    pools = []

    def sb(name, shape, dt):
        g = nc.sbuf_tensor(name, list(shape), dt)
        pools.append(g)
        return g.__enter__()

    def pstile(name, shape, dt):
        g = nc.psum_tensor(name, list(shape), dt)
        pools.append(g)
        return g.__enter__()

    X = sb("X", [128, 4, D], F32)
    big = sb("big", [128, 8192], BF16)
    xnT = sb("xnT", [128, 16, 512], BF16)
    W = [sb("W0", [128, 8192], BF16), sb("W1", [128, 8192], BF16), X[:, 1:3, :].rearrange("p a b -> p (a b)").bitcast(BF16)]
    Wk = sb("Wk", [128, 16, 64], BF16)
    yb = sb("yb", [128, 16, 512], BF16)
    yac = sb("yac", [128, 8, 512], BF16)
    hv = sb("hv", [128, 8 * 514], BF16)
    cqT = sb("cqT", [128, 4, 512], BF16)
    ckT = sb("ckT", [128, 4, 512], BF16)
    kpT = sb("kpT", [128, 2, 2064], BF16)
    qns = [sb("qn0", [128, 512], BF16), sb("qn1", [128, 512], BF16)]; qpes = [sb("qpe0", [128, 512], BF16), sb("qpe1", [128, 512], BF16)]; kcur = sb("kcur", [128, 512], BF16)
    Vg = sb("Vg", [128, 4, 512], BF16)
    KVb = sb("KVb", [128, 6144], BF16)
    NPT = 4
    PT = [sb("PT%d" % i, [128, 512], BF16) for i in range(NPT)]
    sq = sb("sq", [128, 512], BF16)
    rs = sb("rs", [128, 512], F32)
    sqs = [sq, sb("sq1", [128, 512], BF16)]
    rss = [rs, sb("rs1", [128, 512], F32)]
    sqk = ['sq', 'sq1']
    rsk = ['rs', 'rs1']
    kcurs = [kcur, sb("kcur1", [128, 512], BF16), sb("kcur2", [128, 512], BF16)]
    kck = ['kcur', 'kcur1', 'kcur2']
    rot = {'sq': 0, 'rs': 0, 'kcur': 0, 'pts': 0}

    def nxt(name, n):
        i = rot[name] % n
        rot[name] += 1
        return i
    tf = [sb("tf%d" % i, [128, 512], F32) for i in range(2)]
    sg = sb("sg", [128, 4, 512], BF16)
    iot = tf[0][:, 0:128]
    wsraw = tf[1][:, 0:128]
    wsb = sq[:, 0:128]
    trilm = sqs[1][:, 0:128]
    bias32 = tf[1][0:1, 0:512]

    gCS = sb("gCS", [128, 2, 512], F32)
    kst1 = sb("kst1", [128, 4, 64], F32)
    kpb = sb("kpb", [128, 4, 64], BF16)
    sm = sb("sm", [128, 64], F32)
    cst = sb("cst", [128, 2, 8, 2], BF16)
    cst32 = sb("cst32", [128, 8, 2], F32)
    ident = sb("ident", [128, 128], BF16)
    ones = sb("ones", [128, 128], BF16)
    mean128 = sb("mean128", [128, 128], BF16)
    mean64 = sb("mean64", [128, 64], BF16)
    epsc = sb("epsc", [128, 1], F32)
    ngc = sb("ngc", [128, 2, 16], F32)
    avg_bc = sb("avg_bc", [128, 1024], F32)
    qg_bc = sb("qg_bc", [128, 512], F32)
    kvg_bc = sb("kvg_bc", [128, 512], F32)
    krg_bc = sb("krg_bc", [128, 2, 64], F32)
    wsmT = sb("wsmT", [128, 2, 4, 128], BF16)
    biash = sb("biash", [1, 512], BF16)
    biasl = sb("biasl", [1, 512], BF16)
    qngc = sb("qngc", [128, 2], F32); kngc = sb("kngc", [128, 2], F32)
    qr2c = sb("qr2c", [128, 2, 2], F32)
    cwc = sb("cwc", [128, 2, 3, 8], F32)
    ropTM = sb("ropTM", [128, 4, 64], F32)

    print('sbuf bytes remaining', nc.sbuf_bytes_remaining)
    ps = [pstile("ps%d" % i, [128, 512], F32) for i in range(7)]
    psT = pstile("psT", [128, 8, 128], BF16)
    GEN = [0, 1, 2, 5, 6]
    gen_i = [0]

    def gbank():
        i = GEN[gen_i[0] % len(GEN)]
        gen_i[0] += 1
        return i

    op = S.op
    dma = S.dma

    with nc.Block() as block:
        @block.sync
        def _(sync_eng):
            op('pool', lambda e: e.iota(iot, [[1, 128]], base=0, channel_multiplier=-1,
                                        allow_small_or_imprecise_dtypes=True), w=[('tf', 0)])
            op('dve', lambda e: e.tensor_single_scalar(out=ident[:], in_=iot, scalar=0.0, op=ALU.is_equal), r=[('tf', 0)], w=['ident'])
            op('dve', lambda e: e.tensor_single_scalar(out=trilm, in_=iot, scalar=0.0, op=ALU.is_ge), r=[('tf', 0)], w=['sq1'])
            op('dve', lambda e: e.memset(ones[:], 1.0), w=['ones'])
            op('dve', lambda e: e.memset(mean128[:], 1.0 / 128), w=['mean128'])
            op('dve', lambda e: e.memset(mean64[:], 1.0 / 64), w=['mean64'])
            op('dve', lambda e: e.memset(epsc[:], EPS), w=['epsc'])
            dma('sp', ngc[:], norm_gain.rearrange("l (kc p) -> p l kc", p=128), 'ngc', w=['ngc'], nc_ok=True)
            dma('sp', krg_bc[:].rearrange("p l j -> p (l j)"), b_kr_gain.rearrange("l j -> (l j)").partition_broadcast(128), 'krg', w=['krg'])
            dma('sp', qngc[:], b_qn_gain.rearrange("l p -> p l"), 'qngc', w=['qngc'], nc_ok=True)
            dma('sp', kngc[:], b_kn_gain.rearrange("l p -> p l"), 'kngc', w=['kngc'], nc_ok=True)
            dma('sp', qr2c[0:64], qr2.rearrange("l s j -> j l s"), 'qr2c', w=['qr2c'], nc_ok=True)
            for l_ in range(2):
                dma('sp', cwc[:, l_], c_conv_w[l_].rearrange("k (c p) -> p k c", p=128), 'cwc', w=['cwc'], nc_ok=True)
            for l in range(2):
                for g in range(4):
                    dma('sp', wsraw, a_ws[l, g], ('tf', 1), w=[('tf', 1)])
                    op('dve', lambda e: e.tensor_copy(out=wsb, in_=wsraw), r=[('tf', 1)], w=['sq'])
                    op('pe', lambda e: e.transpose(out=psT[:, 0, :], in_=wsb, identity=ident[:]), r=['sq', 'ident'], w=['psT'])
                    op('dve', lambda e, l=l, g=g: e.tensor_tensor(out=wsmT[:, l, g, :], in0=psT[:, 0, :], in1=trilm, op=ALU.mult),
                       r=['psT', 'sq1'], w=['wsmT'])

            blocks = []

            def win_view(l, c0, ncol):
                return w_in[l].rearrange("(kc p) n -> p kc n", p=128)[:, :, c0:c0 + ncol]

            def wslot3(i, k, n):
                return W[i][:, 0:k * n].rearrange("p (k n) -> p k n", n=n)

            cur = {'l': 0, 'first': True}
            wsc_idx = [dict(), dict()]

            def add_block(loads_fn, compute_fn, name):
                blocks.append({'loads': loads_fn, 'compute': compute_fn, 'name': name, 'l': cur['l'], 'first': cur['first']})

            def rstd_small(col_in, col_out, P, n, inv_n, key='sm'):
                op('act', lambda e: e.activation(out=sm[0:P, col_out:col_out + n], in_=sm[0:P, col_in:col_in + n], func=AF.Ln,
                                                 scale=inv_n, bias=epsc[0:P, 0:1]), r=[key, 'epsc'], w=[key])
                op('act', lambda e: e.activation(out=sm[0:P, col_out:col_out + n], in_=sm[0:P, col_out:col_out + n], func=AF.Exp,
                                                 scale=-0.5), r=[key], w=[key])

            def fm_group(bank, lhs_fn, rhs_fn, K, M, T, rkeys, inc_last=True):
                for kc in range(K):
                    last = kc == K - 1
                    op('pe', lambda e, kc=kc, last=last: e.matmul(ps[bank][0:M, 0:T], lhsT=lhs_fn(kc), rhs=rhs_fn(kc),
                                                                    start=(kc == 0), stop=last),
                       r=rkeys, w=[('ps', bank)], inc=last and inc_last)

            def fm_norm(bank, M, T, meanm, gcol, out_ap, out_key):
                si = nxt('sq', 2)
                ri = nxt('rs', 2)
                sq_, rs_ = sqs[si], rss[ri]
                op('act', lambda e: e.activation(out=sq_[0:M, 0:T], in_=ps[bank][0:M, 0:T], func=AF.Square), r=[('ps', bank)], w=[sqk[si]])
                sb_ = gbank()
                op('pe', lambda e: e.matmul(ps[sb_][0:M, 0:T], lhsT=meanm, rhs=sq_[0:M, 0:T], start=True, stop=True),
                   r=[sqk[si], 'mean128', 'mean64'], w=[('ps', sb_)])
                op('act', lambda e: e.activation(out=rs_[0:M, 0:T], in_=ps[sb_][0:M, 0:T], func=AF.Ln, bias=epsc[0:M, 0:1]),
                   r=[('ps', sb_), 'epsc'], w=[rsk[ri]])
                op('act', lambda e: e.activation(out=rs_[0:M, 0:T], in_=rs_[0:M, 0:T], func=AF.Exp, scale=-0.5), r=[rsk[ri]], w=[rsk[ri]])
                if out_ap is not None:
                    op('dve', lambda e: e.scalar_tensor_tensor(out=out_ap, in0=ps[bank][0:M, 0:T], scalar=gcol, in1=rs_[0:M, 0:T],
                                                               op0=ALU.mult, op1=ALU.mult),
                       r=[('ps', bank), rsk[ri], 'qngc', 'kngc'], w=[out_key])
                return ri

            def layer_pass(grp, ti, l, last_layer):
                P = grp == 'P'
                T = 512 if P else 16
                NS = 4 if P else 1
                TS = 128 if P else 16
                gi = 0 if P else 1
                base = ti * 512 if P else 2048
                sidx0 = ti * 4 if P else 16
                nprev = ti if P else 4

                def tok(s):
                    return slice(s * 128, s * 128 + TS)

                def xsb(s):
                    return big[0:TS, s * 2048:(s + 1) * 2048]

                def hT(kc):
                    return big[:, kc * 512:(kc + 1) * 512]

                def hcv(c):
                    return hv[:, c * 514:(c + 1) * 514]

                def vnv(s):
                    return hv[:, s * 1024:(s + 1) * 1024]

                first_merge = [True]
                has_partial = [P and l > 0]

                def phase_norm():
                    for s in range(NS):
                        key = ('smn', s)
                        if has_partial[0]:
                            op('dve', lambda e, s=s: e.reduce_sum(out=sm[0:TS, s:s + 1], in_=sm[0:TS, 24 + s * 4:28 + s * 4], axis=mybir.AxisListType.X),
                               r=[key], w=[key])
                        else:
                            op('act', lambda e, s=s: e.activation(out=xsb(s), in_=X[0:TS, s, :], func=AF.Square, accum_out=sm[0:TS, s:s + 1]),
                               r=[('X', s)], w=['big', key])
                        rstd_small(s, 4 + s, TS, 1, 1.0 / D, key)
                        if s % 2 == 0:
                            op('act', lambda e, s=s: e.activation(out=xsb(s), in_=X[0:TS, s, :], func=AF.Copy, scale=sm[0:TS, 4 + s:5 + s]),
                               r=[('X', s), key], w=['big'])
                        else:
                            op('dve', lambda e, s=s: e.tensor_scalar(out=xsb(s), in0=X[0:TS, s, :], scalar1=sm[0:TS, 4 + s:5 + s], scalar2=1.0,
                                                                     op0=ALU.mult, op1=ALU.mult),
                               r=[('X', s), key], w=['big'])
                    for kc in range(16):
                        for s in range(NS):
                            op('pe', lambda e, s=s, kc=kc: e.transpose(out=psT[:, s, 0:TS], in_=xsb(s)[:, kc * 128:(kc + 1) * 128],
                                                                        identity=ident[0:TS, 0:TS]),
                               r=['big', 'ident'], w=['psT'], inc=(s == NS - 1))
                        op('dve', lambda e, kc=kc: e.tensor_scalar(out=xnT[:, kc, 0:NS * 128].rearrange("p (s t) -> p s t", t=128)[:, :, 0:TS],
                                                                    in0=psT[:, 0:NS, 0:TS], scalar1=ngc[:, l, kc:kc + 1], scalar2=1.0, op0=ALU.mult, op1=ALU.mult),
                           r=['psT', 'ngc'], w=['xnT'])

                def xn_rhs(kc):
                    return xnT[:, kc, 0:T] if P else xnT[:, kc, 0:16]

                def xn_lhs(kc, s):
                    return xnT[:, kc, tok(s)]

                def fm_win_block(l, c0, evac):
                    def loads(i):
                        return [(W[i][:, :], w_in_t[l, WIN_IDX[c0]])]

                    def compute(i):
                        wv = wslot3(i, 16, 512)
                        for j in range(4):
                            bk = gbank()
                            fm_group(bk, lambda kc, j=j: wv[:, kc, j * 128:(j + 1) * 128], xn_rhs, 16, 128, T, [('W', i), 'xnT'])
                            evac(j, bk)
                    add_block(loads, compute, ('win', c0))

                def tm_win_block(l, c0, evac, extra_loads=None, post=None):
                    def loads(i):
                        ls = [(W[i][:, :], w_in_t[l, WIN_IDX[c0]])]
                        return ls

                    def compute(i):
                        wv = wslot3(i, 16, 512)
                        for s in range(NS):
                            bk = gbank()
                            fm_group(bk, lambda kc, s=s: xn_lhs(kc, s), lambda kc: wv[:, kc, :], 16, TS, 512, [('W', i), 'xnT'])
                            evac(s, bk)
                        if post:
                            post()
                    add_block(loads, compute, ('win', c0))

                def merge_branch(k, wp, Kc, yfn, ykeys):
                    fm = first_merge[0]
                    first_merge[0] = False
                    for dmg in range(4):
                        def ev_gate(j, bk):
                            op('act', lambda e: e.activation(out=sg[:, j, 0:T], in_=ps[bk][:, 0:T], func=AF.Sigmoid),
                               r=[('ps', bk)], w=[('sg', j)])
                        fm_win_block(l, C_G + k * 2048 + dmg * 512, ev_gate)

                        def loads(i, dmg=dmg):
                            return [(W[i][:, 0:Kc * 512], wp[l, dmg])]

                        def compute(i, dmg=dmg):
                            wv = wslot3(i, Kc, 512)
                            for j in range(4):
                                n = dmg * 4 + j
                                bk = gbank()
                                fm_group(bk, lambda kc: wv[:, kc, j * 128:(j + 1) * 128], lambda kc: yfn(kc), Kc, 128, T, [('W', i)] + ykeys)
                                if fm:
                                    op('dve', lambda e: e.tensor_tensor(out=hT(n)[:, 0:T], in0=ps[bk][:, 0:T], in1=sg[:, j, 0:T], op=ALU.mult),
                                       r=[('ps', bk), ('sg', j)], w=['big'])
                                else:
                                    t = tf[n % 2]
                                    op('dve', lambda e: e.tensor_tensor(out=t[:, 0:T], in0=ps[bk][:, 0:T], in1=sg[:, j, 0:T], op=ALU.mult),
                                       r=[('ps', bk), ('sg', j)], w=[('tf', n % 2)])
                                    op('dve', lambda e: e.tensor_tensor(out=hT(n)[:, 0:T], in0=hT(n)[:, 0:T], in1=t[:, 0:T], op=ALU.add),
                                       r=[('tf', n % 2), 'big'], w=['big'])
                        add_block(loads, compute, ('p', k, dmg))

                def branch_c():
                    def pre():
                        hc3 = hv[:, :].rearrange("p (c t) -> p c t", t=514)
                        if P and ti == 0:
                            op('dve', lambda e: e.memset(hc3[:, :, 0:2], 0.0), w=['hv'])
                        elif P:
                            op('dve', lambda e: e.tensor_copy(out=hc3[:, :, 0:2], in_=cst[:, l, :, :]), r=['cst'], w=['hv'])
                        else:
                            for t_ in range(2):
                                dma('sp', cst32[:, :, t_], sconv[l, t_].rearrange("(c p) -> p c", p=128), 'cst32', w=['cst32'], nc_ok=True)
                            op('dve', lambda e: e.tensor_copy(out=hc3[:, :, 0:2], in_=cst32[:]), r=['cst32'], w=['hv'])
                    for b in range(2):
                        def ev_ch(j, bk, b=b):
                            c = b * 4 + j
                            op('act', lambda e: e.activation(out=yac[:, c, 0:T], in_=ps[bk][:, 0:T], func=AF.Copy), r=[('ps', bk)], w=[('yac', c)])
                        fm_win_block(l, C_CH + b * 512, ev_ch)
                    for b in range(2):
                        def ev_cc(j, bk, b=b):
                            c = b * 4 + j
                            if c == 0:
                                pre()
                            op('dve', lambda e: e.tensor_tensor(out=hcv(c)[:, 2:2 + T], in0=ps[bk][:, 0:T], in1=yac[:, c, 0:T], op=ALU.mult),
                               r=[('ps', bk), ('yac', c)], w=['hv'])
                            t = tf[c % 2]
                            op('dve', lambda e: e.tensor_scalar(out=t[:, 0:T], in0=hcv(c)[:, 2:2 + T], scalar1=cwc[:, l, 2, c:c + 1], scalar2=1.0, op0=ALU.mult, op1=ALU.mult),
                               r=['hv', 'cwc'], w=[('tf', c % 2)])
                            op('dve', lambda e: e.scalar_tensor_tensor(out=t[:, 0:T], in0=hcv(c)[:, 1:1 + T], scalar=cwc[:, l, 1, c:c + 1], in1=t[:, 0:T],
                                                                       op0=ALU.mult, op1=ALU.add), r=['hv', 'cwc', ('tf', c % 2)], w=[('tf', c % 2)])
                            op('dve', lambda e: e.scalar_tensor_tensor(out=yac[:, c, 0:T], in0=hcv(c)[:, 0:T], scalar=cwc[:, l, 0, c:c + 1], in1=t[:, 0:T],
                                                                       op0=ALU.mult, op1=ALU.add), r=['hv', 'cwc', ('tf', c % 2)], w=[('yac', c)])
                            if c == 7:
                                hc3 = hv[:, :].rearrange("p (c t) -> p c t", t=514)
                                if P:
                                    op('dve', lambda e: e.tensor_copy(out=cst[:, l, :, :], in_=hc3[:, :, T:T + 2]), r=['hv'], w=['cst'])
                                if (P and ti == nptiles - 1 and nptiles == 4) or not P:
                                    op('dve', lambda e: e.tensor_copy(out=cst32[:], in_=hc3[:, :, T:T + 2]), r=['hv'], w=['cst32'])
                                    for t_ in range(2):
                                        dst = (o_conv_p if P else o_conv_s)[l, t_].rearrange("(c p) -> p c", p=128)
                                        dma('sp', dst, cst32[:, :, t_], 'cst32', r=['cst32'], nc_ok=True)
                        fm_win_block(l, C_CC + b * 512, ev_cc)
                    for b in range(2):
                        def ev_cz(j, bk, b=b):
                            c = b * 4 + j
                            op('act', lambda e: e.activation(out=sq[:, 0:T], in_=ps[bk][:, 0:T], func=AF.Silu), r=[('ps', bk)], w=['sq'])
                            op('dve', lambda e: e.tensor_tensor(out=yac[:, c, 0:T], in0=yac[:, c, 0:T], in1=sq[:, 0:T], op=ALU.mult),
                               r=['sq', ('yac', c)], w=[('yac', c)])
                        fm_win_block(l, C_CZ + b * 512, ev_cz)
                    for b in range(2):
                        def ev_cb(j, bk, b=b):
                            c = b * 4 + j
                            op('dve', lambda e: e.tensor_tensor(out=yac[:, c, 0:T], in0=ps[bk][:, 0:T], in1=yac[:, c, 0:T], op=ALU.mult),
                               r=[('ps', bk), ('yac', c)], w=[('yac', c)])
                        fm_win_block(l, C_CB + b * 512, ev_cb)
                    merge_branch(2, w_pc, 8, lambda kc: yac[:, kc, 0:T], [('yac', c) for c in range(8)])

                def branch_a():
                    for b in range(2):
                        def ev_az(j, bk, b=b):
                            c = b * 4 + j
                            op('act', lambda e: e.activation(out=yac[:, c, 0:T], in_=ps[bk][:, 0:T], func=AF.Silu), r=[('ps', bk)], w=[('yac', c)])
                        fm_win_block(l, C_AZ + b * 512, ev_az)
                    for b in range(2):
                        def ev_au(j, bk, b=b):
                            c = b * 4 + j
                            op('dve', lambda e: e.tensor_tensor(out=yac[:, c, 0:T], in0=ps[bk][:, 0:T], in1=yac[:, c, 0:T], op=ALU.mult),
                               r=[('ps', bk), ('yac', c)], w=[('yac', c)])
                        fm_win_block(l, C_AU + b * 512, ev_au)

                    def mixing():
                        for c in range(8):
                            g = c // 2
                            bk = gbank()
                            for s in range(NS):
                                o_ = ps[bk][:, tok(s)]
                                op('pe', lambda e, s=s: e.matmul(o_, lhsT=vnv(s)[0:TS, c * 128:(c + 1) * 128], rhs=wsmT[0:TS, l, g, 0:TS], start=True, stop=False),
                                   r=['hv', 'wsmT'], w=[('ps', bk)], inc=False)
                                op('pe', lambda e: e.matmul(o_, lhsT=ones[0:1, :], rhs=biash[0:1, g * 128:g * 128 + TS], start=False, stop=False),
                                   r=['ones', 'biash'], w=[('ps', bk)], inc=False)
                                op('pe', lambda e: e.matmul(o_, lhsT=ones[0:1, :], rhs=biasl[0:1, g * 128:g * 128 + TS], start=False, stop=True),
                                   r=['ones', 'biasl'], w=[('ps', bk)], inc=(s == NS - 1))
                            op('dve', lambda e: e.tensor_tensor(out=yac[:, c, 0:T], in0=ps[bk][:, 0:T], in1=yac[:, c, 0:T], op=ALU.mult),
                               r=[('ps', bk), ('yac', c)], w=[('yac', c)])

                    for b in range(2):
                        def ev_av(s, bk, b=b):
                            for gl in range(2):
                                op('act', lambda e, gl=gl: e.activation(out=sq[0:TS, 0:256], in_=ps[bk][0:TS, gl * 256:(gl + 1) * 256], func=AF.Square,
                                                                       accum_out=sm[0:TS, 8 + gl:9 + gl]), r=[('ps', bk)], w=['sq', 'sma'])
                            rstd_small(8, 10, TS, 2, 1.0 / 256, 'sma')
                            for gl in range(2):
                                cs = slice(b * 512 + gl * 256, b * 512 + (gl + 1) * 256)
                                if P:
                                    op('dve', lambda e, gl=gl, cs=cs: e.scalar_tensor_tensor(out=vnv(s)[0:TS, cs], in0=ps[bk][0:TS, gl * 256:(gl + 1) * 256],
                                                                                         scalar=sm[0:TS, 10 + gl:11 + gl], in1=avg_bc[0:TS, cs], op0=ALU.mult, op1=ALU.mult),
                                       r=[('ps', bk), 'sma', 'avg'], w=['hv'])
                                else:
                                    t = tf[gl]
                                    op('dve', lambda e, gl=gl, cs=cs: e.scalar_tensor_tensor(out=t[0:TS, 0:256], in0=ps[bk][0:TS, gl * 256:(gl + 1) * 256],
                                                                                         scalar=sm[0:TS, 10 + gl:11 + gl], in1=avg_bc[0:TS, cs], op0=ALU.mult, op1=ALU.mult),
                                       r=[('ps', bk), 'sma', 'avg'], w=[('tf', gl)])
                                    op('act', lambda e, cs=cs: e.activation(out=vnv(s)[0:TS, cs], in_=t[0:TS, 0:256], func=AF.Copy), r=[('tf', gl)], w=['hv'])
                                    dma('sp', o_av_s[l][:, cs], t[0:TS, 0:256], ('tf', gl), r=[('tf', gl)])
                        tm_win_block(l, C_AV + b * 512, ev_av, post=(mixing if b == 1 else None))
                    merge_branch(0, w_pa, 8, lambda kc: yac[:, kc, 0:T], [('yac', c) for c in range(8)])

                def kv_expand(i, g, ckT_cols, TT, NS_, TS_, tile_idx, grp_i, store, ck=None, ckk='ckT', q='sp', vout=None):
                    ck = ckT if ck is None else ck
                    wq = W[i][:, :].rearrange("p (k h c) -> p k h c", k=4, h=4)
                    for s in range(NS_):
                        bk = gbank()
                        fm_group(bk, lambda kc, s=s: ck[:, kc, s * 128:s * 128 + TS_], lambda kc: wq[:, kc, :, 384:512], 4, TS_, 512, [('W', i), ckk])
                        if vout is not None:
                            op('act', lambda e, s=s: e.activation(out=vout[0][0:TS_, s, :], in_=ps[bk][0:TS_, :], func=AF.Copy), r=[('ps', bk)], w=[vout[1][s]])
                        else:
                            op('act', lambda e, s=s: e.activation(out=Vg[0:TS_, s, :], in_=ps[bk][0:TS_, :], func=AF.Copy), r=[('ps', bk)], w=['Vg'])
                    if store:
                        dst = Vsc[grp_i, l, g, tile_idx * 4:(tile_idx + 1) * 4].rearrange("s p c -> p s c")
                        dma(q, dst, Vg[:, :, :], 'Vg', r=['Vg'], w=[('Vsc', grp_i, l, g)])

                def k_head(i, g, hl, TT, tile_idx, grp_i, store, ck=None, ckk='ckT', q='sp', out=None):
                    ck = ckT if ck is None else ck
                    wq = W[i][:, :].rearrange("p (k h c) -> p k h c", k=4, h=4)
                    h = g * 4 + hl
                    bk = gbank()
                    fm_group(bk, lambda kc: wq[:, kc, hl, 256:384], lambda kc: ck[:, kc, 0:TT], 4, 128, TT, [('W', i), ckk])
                    if out is not None:
                        fm_norm(bk, 128, TT, mean128[:, :], kngc[:, l:l + 1], out[0], out[1])
                        return None
                    ki = nxt('kcur', 3)
                    fm_norm(bk, 128, TT, mean128[:, :], kngc[:, l:l + 1], kcurs[ki][:, 0:TT], kck[ki])
                    if store:
                        dma(q, Ksc[grp_i, l, h, :, tile_idx * 512:(tile_idx + 1) * 512], kcurs[ki][:, 0:TT], kck[ki], r=[kck[ki]], w=[('Ksc', grp_i, l, h)])
                    return ki

                def cexp_items():
                    items = []
                    XB3 = X[:, 3, :].bitcast(BF16)
                    stg = XB3[:, 0:2048].rearrange("p (s c) -> p s c", c=512)
                    ckbufs = [(XB3[:, 2048:4096].rearrange("p (k t) -> p k t", k=4), ('X3', 1)), (ckT, 'ckT')]
                    ucount = [0]

                    def load_w2(g):
                        def f():
                            idx = wsc_idx[l][('qkv', g)]
                            dma('sp', W[2][:, :], Wsc[l, idx], ('W', 2), r=[('Wsc', l, idx)], w=[('W', 2)])
                        return f

                    pending = []

                    def flush():
                        for fn in pending:
                            fn()
                        pending[:] = []

                    def unit(g, c4):
                        def f():
                            u = ucount[0]
                            ck, ckk = ckbufs[u % 2]
                            ucount[0] += 1
                            K4 = yb[:, (u % 2) * 4:(u % 2) * 4 + 4, :]
                            K4k = [('yb', (u % 2) * 4 + j) for j in range(4)]
                            V4 = yb[:, 8 + (u % 2) * 4:12 + (u % 2) * 4, :]
                            V4k = [('yb', 8 + (u % 2) * 4 + j) for j in range(4)]
                            dma('pool', stg, cckv[l, c4 * 512:(c4 + 1) * 512, :].rearrange("(s p) c -> p s c", p=128), 'x3stg', w=[('X3', 0)])
                            flush()
                            for kc in range(4):
                                for s in range(4):
                                    op('pe', lambda e, s=s, kc=kc: e.transpose(out=psT[:, s, :], in_=stg[:, s, kc * 128:(kc + 1) * 128], identity=ident[:, :]),
                                       r=[('X3', 0), 'ident'], w=['psT'], inc=(s == 3))
                                op('dve', lambda e, kc=kc: e.tensor_copy(out=ck[:, kc, :].rearrange("p (s t) -> p s t", t=128), in_=psT[:, 0:4, :]),
                                   r=['psT'], w=[ckk])
                            if g == 0:
                                dma('pool', kpb[:, :, :], ckpe[l, c4 * 512:(c4 + 1) * 512, :].rearrange("(s p) c -> p s c", p=128), 'kpbstg', w=['kpb'])
                                for s in range(4):
                                    op('pe', lambda e, s=s: e.transpose(out=psT[0:64, s, :], in_=kpb[:, s, :], identity=ident[:, :]),
                                       r=['kpb', 'ident'], w=['psT'], inc=(s == 3))
                                op('dve', lambda e: e.tensor_copy(out=kpT[0:64, l, c4 * 512:(c4 + 1) * 512].rearrange("p (s t) -> p s t", t=128),
                                                                  in_=psT[0:64, 0:4, :]), r=['psT'], w=['kpT'])
                            kv_expand(2, g, None, 512, 4, 128, c4, 1, False, ck=ck, ckk=ckk, vout=(V4, V4k))
                            for hl in range(4):
                                k_head(2, g, hl, 512, c4, 1, False, ck=ck, ckk=ckk, out=(K4[:, hl, :], K4k[hl]))

                            def stores(u=u, g=g, c4=c4, K4=K4, K4k=K4k, V4=V4, V4k=V4k):
                                dma('pool', Vsc[1, l, g, c4 * 4:(c4 + 1) * 4].rearrange("s p c -> p s c"), V4, ('ybV', u % 2), r=V4k, w=[('Vsc', 1, l, g)])
                                dma('pool', Ksc[1, l, g * 4:(g + 1) * 4, :, c4 * 512:(c4 + 1) * 512].rearrange("h d t -> d h t"), K4, ('ybK', u % 2), r=K4k,
                                    w=[('Ksc', 1, l, g * 4 + j) for j in range(4)])
                            pending.append(stores)
                        return f
                    for g in range(4):
                        items.append(load_w2(g))
                        for c4 in range(4):
                            items.append(unit(g, c4))
                    items.append(flush)
                    return items

                def branch_b():
                    def qkv_loads(g):
                        def loads(i):
                            return [(W[i][:, :], w_qkv_t[l, g])]
                        return loads

                    def lat_block(which, wv, gain_bc, gkey, dstT, dkey, out_fn):
                        banks = []
                        col0 = 44 if which == 'q' else 52

                        def mm(s):
                            bk = gbank()
                            banks.append(bk)
                            fm_group(bk, lambda kc, s=s: xn_lhs(kc, s), lambda kc: wv[:, kc, :], 16, TS, 512, ['xnT'] + wkeys_cur)

                        def chain(s):
                            bk = banks[s]
                            si = s % 2
                            stg = sqs[si]
                            col = col0 + 2 * s
                            skey = ('sml', which, s)
                            op('act', lambda e: e.activation(out=stg[0:TS, :], in_=ps[bk][0:TS, :], func=AF.Square, accum_out=sm[0:TS, col:col + 1]),
                               r=[('ps', bk)], w=[sqk[si], skey])
                            rstd_small(col, col + 1, TS, 1, 1.0 / 512, skey)
                            if out_fn is None:
                                op('dve', lambda e: e.scalar_tensor_tensor(out=stg[0:TS, :], in0=ps[bk][0:TS, :], scalar=sm[0:TS, col + 1:col + 2], in1=gain_bc[0:TS, :],
                                                                           op0=ALU.mult, op1=ALU.mult), r=[('ps', bk), skey, gkey], w=[sqk[si]])
                            else:
                                t = tf[s % 2]
                                op('dve', lambda e: e.scalar_tensor_tensor(out=t[0:TS, :], in0=ps[bk][0:TS, :], scalar=sm[0:TS, col + 1:col + 2], in1=gain_bc[0:TS, :],
                                                                           op0=ALU.mult, op1=ALU.mult), r=[('ps', bk), skey, gkey], w=[('tf', s % 2)])
                                out_fn(s, t)
                                op('act', lambda e: e.activation(out=stg[0:TS, :], in_=t[0:TS, :], func=AF.Copy), r=[('tf', s % 2)], w=[sqk[si]])

                        def tr(s):
                            si = s % 2
                            stg = sqs[si]
                            for c in range(4):
                                op('pe', lambda e, c=c: e.transpose(out=psT[:, c, 0:TS], in_=stg[0:TS, c * 128:(c + 1) * 128], identity=ident[0:TS, 0:TS]),
                                   r=[sqk[si], 'ident'], w=['psT'], inc=(c == 3))
                            op('dve', lambda e: e.tensor_copy(out=dstT[:, :, tok(s)], in_=psT[:, 0:4, 0:TS]), r=['psT'], w=[dkey])

                        if NS == 4:
                            mm(0); mm(1); chain(0); mm(2); chain(1); tr(0); mm(3); chain(2); tr(1); chain(3); tr(2); tr(3)
                        else:
                            mm(0); chain(0); tr(0)

                    wkeys_cur = []

                    def compute_cq(i):
                        wkeys_cur[:] = [('W', i)]
                        lat_block('q', wslot3(i, 16, 512), qg_bc, 'qg', cqT, 'cqT', None)
                    add_block(lambda i: [(W[i][:, :], w_in_t[l, WIN_IDX[C_CQ]])], compute_cq, ('win', C_CQ))

                    def ckv_out(s, t):
                        dst = o_ckv_p[l, base + s * 128:base + s * 128 + TS, :] if P else o_ckv_s[l]
                        dma('sp', dst, t[0:TS, :], ('tf', s % 2), r=[('tf', s % 2)])

                    def kpe_mm():
                        bk = gbank()
                        for s in range(NS):
                            for kc in range(16):
                                last = kc == 15
                                op('pe', lambda e, s=s, kc=kc, last=last: e.matmul(ps[bk][0:TS, s * 64:(s + 1) * 64], lhsT=xn_lhs(kc, s), rhs=Wk[:, kc, :],
                                                                                  start=(kc == 0), stop=last),
                                   r=['Wk', 'xnT'], w=[('ps', bk)], inc=(last and s == NS - 1))
                        return bk

                    def kpe_chain(bk):
                        pk = ps[bk][0:TS, 0:NS * 64].rearrange("p (s j) -> p s j", j=64)
                        for s in range(NS):
                            op('act', lambda e, s=s: e.activation(out=sq[0:TS, 0:64], in_=pk[:, s, :], func=AF.Square, accum_out=sm[0:TS, 20 + s:21 + s]),
                               r=[('ps', bk)], w=['sq', 'smp'])
                        rstd_small(20, 40, TS, NS, 1.0 / 64, 'smp')
                        kn = rss[1][0:TS, 0:NS * 64].rearrange("p (s j) -> p s j", j=64)
                        ko = kst1[0:TS, 0:NS, :]
                        tt = rss[0][0:TS, 0:NS * 64].rearrange("p (s j) -> p s j", j=64)
                        for s in range(NS):
                            op('dve', lambda e, s=s: e.scalar_tensor_tensor(out=kn[:, s, :], in0=pk[:, s, :], scalar=sm[0:TS, 40 + s:41 + s], in1=krg_bc[0:TS, l, :],
                                                                           op0=ALU.mult, op1=ALU.mult), r=[('ps', bk), 'smp', 'krg'], w=['rs1'])
                        C_ = ropTM[0:TS, 0:NS, 0:32]
                        S_ = ropTM[0:TS, 0:NS, 32:64]
                        op('dve', lambda e: e.tensor_tensor(out=tt[:, :, 0:32], in0=kn[:, :, 0:32], in1=C_, op=ALU.mult), r=['rs1', 'ropTM'], w=['rs'])
                        op('dve', lambda e: e.tensor_tensor(out=tt[:, :, 32:64], in0=kn[:, :, 32:64], in1=S_, op=ALU.mult), r=['rs1', 'ropTM'], w=['rs'])
                        op('dve', lambda e: e.tensor_tensor(out=ko[:, :, 0:32], in0=tt[:, :, 0:32], in1=tt[:, :, 32:64], op=ALU.subtract), r=['rs'], w=['kst1'])
                        op('dve', lambda e: e.tensor_tensor(out=tt[:, :, 0:32], in0=kn[:, :, 32:64], in1=C_, op=ALU.mult), r=['rs1', 'ropTM', 'kst1'], w=['rs'])
                        op('dve', lambda e: e.tensor_tensor(out=tt[:, :, 32:64], in0=kn[:, :, 0:32], in1=S_, op=ALU.mult), r=['rs1', 'ropTM'], w=['rs'])
                        op('dve', lambda e: e.tensor_tensor(out=ko[:, :, 32:64], in0=tt[:, :, 0:32], in1=tt[:, :, 32:64], op=ALU.add), r=['rs'], w=['kst1'])
                        if P:
                            dma('sp', o_kpe_p[l, base:base + 512, :].rearrange("(s p) j -> p s j", p=128), ko, 'kst1', r=['kst1'])
                        else:
                            dma('sp', o_kpe_s[l], kst1[0:TS, 0, :], 'kst1', r=['kst1'])
                        op('act', lambda e: e.activation(out=kpb[0:TS, 0:NS, :], in_=ko, func=AF.Copy), r=['kst1'], w=['kpb'])

                    def kpe_tr():
                        for s in range(NS):
                            op('pe', lambda e, s=s: e.transpose(out=psT[0:64, s, 0:TS], in_=kpb[0:TS, s, :], identity=ident[0:TS, 0:TS]),
                               r=['kpb', 'ident'], w=['psT'], inc=(s == NS - 1))
                        if P:
                            op('dve', lambda e: e.tensor_copy(out=kpT[0:64, l, base:base + 512].rearrange("p (s t) -> p s t", t=128),
                                                              in_=psT[0:64, 0:4, :]), r=['psT'], w=['kpT'])
                        else:
                            op('dve', lambda e: e.tensor_copy(out=kpT[0:64, l, 2048:2064], in_=psT[0:64, 0, 0:16]), r=['psT'], w=['kpT'])

                    def loads_ckv(i):
                        return [(W[i][:, :], w_in_t[l, WIN_IDX[C_CKV]]), (Wk[:, :, :], w_kpe[l].rearrange("(kc p) n -> p kc n", p=128), 'Wk')]

                    def compute_ckv(i):
                        wkeys_cur[:] = [('W', i)]
                        kb = kpe_mm()
                        kpe_chain(kb)
                        lat_block('kv', wslot3(i, 16, 512), kvg_bc, 'kvg', ckT, 'ckT', ckv_out)
                        kpe_tr()
                    add_block(loads_ckv, compute_ckv, ('win', C_CKV))

                    for b in range(4):
                        def ev_bz(j, bk, b=b):
                            n = b * 4 + j
                            op('act', lambda e: e.activation(out=yb[:, n, 0:T], in_=ps[bk][:, 0:T], func=AF.Silu), r=[('ps', bk)], w=[('yb', n)])
                        fm_win_block(l, C_BZ + b * 512, ev_bz)

                    def attn_group(g):
                        def compute(i):
                            wq = W[i][:, :].rearrange("p (k h c) -> p k h c", k=4, h=4)
                            store = P and (ti < 3)
                            ob, db = 3, 4
                            kv_expand(i, g, None, T, NS, TS, ti, gi, store)

                            def kvbufs(hl):
                                if P:
                                    j = hl % 2
                                    Kp_ = KVb[:, j * 3072:j * 3072 + 1536]
                                    Vp_ = KVb[:, j * 3072 + 1536:(j + 1) * 3072].rearrange("p (s d) -> p s d", d=128)
                                    return Kp_, Vp_, [('KVk', j)], [('KVv', j)], ('Kp', j), ('Vp', j)
                                if hl % 2 == 0:
                                    Kp_ = KVb[:, 0:2048]
                                    Vp_ = KVb[:, 2048:4096].rearrange("p (s d) -> p s d", d=128)
                                    return Kp_, Vp_, [('KVk', 0), ('KVv', 0)], [('KVv', 0), ('KVk', 1)], ('Kp', 0), ('Vp', 0)
                                Kp_ = hv[:, 0:2048]
                                Vp_ = yac[:, 0:4, :].rearrange("p c (s d) -> p (c s) d", d=128)
                                return Kp_, Vp_, ['hv'], [('yac', c) for c in range(4)], ('Kp', 1), ('Vp', 1)

                            def prefetch(hl):
                                h = g * 4 + hl
                                if nprev > 0:
                                    Kp_, Vp_, kk, vk, ks_, vs_ = kvbufs(hl)
                                    dma('sp', Kp_[:, 0:nprev * 512], Ksc[gi, l, h, :, 0:nprev * 512], ks_, r=[('Ksc', gi, l, h)], w=kk)
                                    dma('sp', Vp_[:, 0:nprev * 4, :], Vsc[gi, l, g, 0:nprev * 4, :, hl * 128:(hl + 1) * 128].rearrange("s p d -> p s d"), vs_,
                                        r=[('Vsc', gi, l, g)], w=vk)

                            def chain(hl):
                                h = g * 4 + hl
                                qn = qns[h % 2]; qpe = qpes[h % 2]
                                qnk = 'qn%d' % (h % 2); qpk = 'qpe%d' % (h % 2)
                                bk = gbank()
                                fm_group(bk, lambda kc: wq[:, kc, hl, 0:128], lambda kc: cqT[:, kc, 0:T], 4, 128, T, [('W', i), 'cqT'])
                                bA = gbank()
                                fm_group(bA, lambda kc: wq[:, kc, hl, 128:192], lambda kc: cqT[:, kc, 0:T], 4, 64, T, [('W', i), 'cqT'])
                                fm_norm(bk, 128, T, mean128[:, :], qngc[:, l:l + 1], qn[:, 0:T], qnk)
                                bB = gbank()
                                fm_group(bB, lambda kc: wq[:, kc, hl, 192:256], lambda kc: cqT[:, kc, 0:T], 4, 64, T, [('W', i), 'cqT'])
                                rpi = fm_norm(bA, 64, T, mean64[0:64, :], None, None, None)
                                op('dve', lambda e: e.tensor_tensor(out=tf[0][0:64, 0:T], in0=ps[bA][0:64, 0:T], in1=gCS[0:64, 0, 0:T], op=ALU.mult),
                                   r=[('ps', bA), 'gCS'], w=[('tf', 0)])
                                op('dve', lambda e: e.tensor_tensor(out=tf[1][0:64, 0:T], in0=ps[bB][0:64, 0:T], in1=gCS[0:64, 1, 0:T], op=ALU.mult),
                                   r=[('ps', bB), 'gCS'], w=[('tf', 1)])
                                op('dve', lambda e: e.tensor_tensor(out=tf[0][0:64, 0:T], in0=tf[0][0:64, 0:T], in1=tf[1][0:64, 0:T], op=ALU.add),
                                   r=[('tf', 0), ('tf', 1)], w=[('tf', 0)])
                                op('dve', lambda e: e.tensor_tensor(out=qpe[0:64, 0:T], in0=tf[0][0:64, 0:T], in1=rss[rpi][0:64, 0:T], op=ALU.mult),
                                   r=[('tf', 0), rsk[rpi]], w=[qpk])
                                kci = k_head(i, g, hl, T, ti, gi, store)
                                return kci

                            def loop(hl, kci):
                                h = g * 4 + hl
                                qn = qns[h % 2]; qpe = qpes[h % 2]
                                qnk = 'qn%d' % (h % 2); qpk = 'qpe%d' % (h % 2)
                                kcur_ = kcurs[kci]
                                Kp_, Vp_, kk, vk, _, _ = kvbufs(hl)
                                kts = []
                                for j in range(nprev * 4):
                                    kts.append((Kp_[:, j * 128:(j + 1) * 128], kpT[0:64, l, j * 128:(j + 1) * 128], Vp_[:, j, :], 128, 0, False, kk + ['kpT'], vk))
                                for s in range(NS):
                                    kts.append((kcur_[:, tok(s)], kpT[0:64, l, base + s * 128:base + s * 128 + TS], Vg[0:TS, s, hl * 128:(hl + 1) * 128], TS,
                                                (s * 128 if P else 0), P, [kck[kci], 'kpT'], ['Vg']))
                                nk_ = len(kts)

                                def pv(ix):
                                    kt = kts[ix]
                                    pt = PT[ix % NPT]
                                    nk, c0 = kt[3], kt[4]
                                    op('pe', lambda e: e.matmul(ps[ob][:, c0:T], lhsT=kt[2], rhs=pt[0:nk, c0:T], start=(ix == 0), stop=(ix == nk_ - 1)),
                                       r=[('PT', ix % NPT)] + kt[7], w=[('ps', ob)], inc=False)
                                    op('pe', lambda e: e.matmul(ps[db][:, c0:T], lhsT=ones[0:nk, :], rhs=pt[0:nk, c0:T], start=(ix == 0), stop=(ix == nk_ - 1)),
                                       r=[('PT', ix % NPT), 'ones'], w=[('ps', db)], inc=True)

                                if not P:
                                    bk = gbank()
                                    pti = nxt('pts', NPT)
                                    pt = PT[pti]
                                    for ix, kt in enumerate(kts):
                                        nk = kt[3]
                                        cs = slice(ix * 16, ix * 16 + 16)
                                        op('pe', lambda e: e.matmul(ps[bk][0:nk, cs], lhsT=kt[0], rhs=qn[:, 0:16], start=True, stop=False),
                                           r=kt[6] + [qnk], w=[('ps', bk)], inc=False)
                                        op('pe', lambda e: e.matmul(ps[bk][0:nk, cs], lhsT=kt[1], rhs=qpe[0:64, 0:16], start=False, stop=True),
                                           r=kt[6] + [qpk], w=[('ps', bk)], inc=(ix == nk_ - 1))
                                    npv = nk_ - 1
                                    op('act', lambda e: e.activation(out=pt[:, 0:npv * 16], in_=ps[bk][:, 0:npv * 16], func=AF.Exp, scale=SCALE),
                                       r=[('ps', bk)], w=[('PT', pti)])
                                    op('act', lambda e: e.activation(out=pt[0:16, npv * 16:nk_ * 16], in_=ps[bk][0:16, npv * 16:nk_ * 16], func=AF.Exp, scale=SCALE),
                                       r=[('ps', bk)], w=[('PT', pti)])
                                    for ix, kt in enumerate(kts):
                                        nk = kt[3]
                                        cs = slice(ix * 16, ix * 16 + 16)
                                        op('pe', lambda e: e.matmul(ps[ob][:, 0:16], lhsT=kt[2], rhs=pt[0:nk, cs], start=(ix == 0), stop=(ix == nk_ - 1)),
                                           r=[('PT', pti)] + kt[7], w=[('ps', ob)], inc=False)
                                        op('pe', lambda e: e.matmul(ps[db][:, 0:16], lhsT=ones[0:nk, :], rhs=pt[0:nk, cs], start=(ix == 0), stop=(ix == nk_ - 1)),
                                           r=[('PT', pti), 'ones'], w=[('ps', db)], inc=(ix == nk_ - 1))
                                else:
                                    for ix, kt in enumerate(kts):
                                        nk, c0, diag = kt[3], kt[4], kt[5]
                                        bk = gbank()
                                        pt = PT[ix % NPT]
                                        op('pe', lambda e: e.matmul(ps[bk][0:nk, c0:T], lhsT=kt[0], rhs=qn[:, c0:T], start=True, stop=False),
                                           r=kt[6] + [qnk], w=[('ps', bk)], inc=False)
                                        op('pe', lambda e: e.matmul(ps[bk][0:nk, c0:T], lhsT=kt[1], rhs=qpe[0:64, c0:T], start=False, stop=True),
                                           r=kt[6] + [qpk], w=[('ps', bk)], inc=True)
                                        if diag:
                                            op('act', lambda e: e.activation(out=pt[0:64, c0:T], in_=ps[bk][0:64, c0:T], func=AF.Exp, scale=SCALE),
                                               r=[('ps', bk)], w=[('PT', ix % NPT)])
                                            op('act', lambda e: e.activation(out=pt[64:128, c0 + 64:T], in_=ps[bk][64:128, c0 + 64:T], func=AF.Exp, scale=SCALE),
                                               r=[('ps', bk)], w=[('PT', ix % NPT)])
                                            op('dve', lambda e: e.memset(pt[64:128, c0:c0 + 64], 0.0), w=[('PT', ix % NPT)])
                                        else:
                                            op('act', lambda e: e.activation(out=pt[0:nk, c0:T], in_=ps[bk][0:nk, c0:T], func=AF.Exp, scale=SCALE),
                                               r=[('ps', bk)], w=[('PT', ix % NPT)])
                                        if ix >= LA:
                                            pv(ix - LA)
                                    for ix in range(max(0, nk_ - LA), nk_):
                                        pv(ix)
                                rfi = nxt('rs', 2)
                                op('dve', lambda e: e.reciprocal(out=rss[rfi][:, 0:T], in_=ps[db][:, 0:T]), r=[('ps', db)], w=[rsk[rfi]])
                                op('dve', lambda e: e.tensor_tensor(out=tf[0][:, 0:T], in0=ps[ob][:, 0:T], in1=rss[rfi][:, 0:T], op=ALU.mult),
                                   r=[('ps', ob), rsk[rfi]], w=[('tf', 0)])
                                op('dve', lambda e: e.tensor_tensor(out=yb[:, h, 0:T], in0=yb[:, h, 0:T], in1=tf[0][:, 0:T], op=ALU.mult),
                                   r=[('tf', 0), ('yb', h)], w=[('yb', h)])

                            prefetch(0)
                            kci_next = chain(0)
                            for hl in range(4):
                                kci = kci_next
                                if hl < 3:
                                    prefetch(hl + 1)
                                    kci_next = chain(hl + 1)
                                loop(hl, kci)
                        return compute

                    def b_front():
                        pass
                    for g in range(4):
                        add_block(qkv_loads(g), attn_group(g), ('qkv', g))
                    merge_branch(1, w_pb, 16, lambda kc: yb[:, kc, 0:T], [('yb', n) for n in range(16)])

                def out_proj():
                    for dmb in range(4):
                        def loads(i, dmb=dmb):
                            return [(W[i][:, :], w_out[l, dmb])]

                        def compute(i, dmb=dmb):
                            wv = wslot3(i, 16, 512)
                            for s in range(NS):
                                bk = gbank()
                                fm_group(bk, lambda kc, s=s: hT(kc)[:, tok(s)], lambda kc: wv[:, kc, :], 16, TS, 512, [('W', i), 'big'])
                                op('dve', lambda e, s=s: e.tensor_tensor(out=X[0:TS, s, dmb * 512:(dmb + 1) * 512], in0=ps[bk][0:TS, :],
                                                                        in1=X[0:TS, s, dmb * 512:(dmb + 1) * 512], op=ALU.add),
                                   r=[('ps', bk), ('X', s)], w=[('X', s)])
                                if P and not last_layer:
                                    op('act', lambda e, s=s: e.activation(out=sq[0:TS, :], in_=X[0:TS, s, dmb * 512:(dmb + 1) * 512], func=AF.Square,
                                                                          accum_out=sm[0:TS, 24 + s * 4 + dmb:25 + s * 4 + dmb]),
                                       r=[('X', s)], w=['sq', ('smn', s)])
                                if last_layer and dmb == 3:
                                    dst = yp[base + s * 128:base + s * 128 + TS, :] if P else ys[:, :]
                                    dma('sp', dst, X[0:TS, s, :], ('X', s), r=[('X', s)])
                        add_block(loads, compute, ('out', dmb))

                def front(i):
                    if not P and l == 0:
                        op('dve', lambda e: e.memset(X[:, 3, 0:1], 0.0), w=[('X', 3), ('X3', 0), ('X3', 1)])
                        op('dve', lambda e: e.memset(X[:, 1, 0:1], 0.0), w=[('X', 1), ('X', 2), ('W', 2)])
                    dma('sp', avg_bc[:], a_v_gain[l].partition_broadcast(128), 'avg', w=['avg'])
                    dma('sp', qg_bc[:], b_q_gain[l].partition_broadcast(128), 'qg', w=['qg'])
                    dma('sp', kvg_bc[:], b_kv_gain[l].partition_broadcast(128), 'kvg', w=['kvg'])
                    dma('sp', bias32, a_bias[l], ('tf', 1), w=[('tf', 1)])
                    op('dve', lambda e: e.tensor_copy(out=biash[:], in_=bias32), r=[('tf', 1)], w=['biash'])
                    op('dve', lambda e: e.tensor_tensor(out=bias32, in0=bias32, in1=biash[:], op=ALU.subtract), r=['biash', ('tf', 1)], w=[('tf', 1)])
                    op('dve', lambda e: e.tensor_copy(out=biasl[:], in_=bias32), r=[('tf', 1)], w=['biasl'])
                    dma('sp', gCS[0:64, :, 0:T], rope_fm[:, :, base:base + T].rearrange("c j t -> j c t"), 'gCS', w=['gCS'])
                    if l == 0:
                        if P:
                            dma('sp', ropTM[:, :, :], rope_tm[base:base + 512, :].rearrange("(i p) j -> p i j", p=128), 'ropTM', w=['ropTM'])
                        else:
                            dma('sp', ropTM[0:16, 0, :], rope_tm[2048:2064, :], 'ropTM', w=['ropTM'])
                    for c in range(2):
                        op('dve', lambda e, c=c: e.tensor_scalar(out=gCS[0:64, c, 0:T], in0=gCS[0:64, c, 0:T], scalar1=qr2c[0:64, l, c:c + 1], scalar2=1.0,
                                                                 op0=ALU.mult, op1=ALU.mult),
                           r=['gCS', 'qr2c'], w=['gCS'])
                    phase_norm()

                nb0 = len(blocks)
                items = []
                if not P:
                    items = cexp_items()
                branch_c()
                cf = blocks[nb0]['compute']
                blocks[nb0]['compute'] = (lambda i, cf=cf: (front(i), cf(i)))
                branch_a()
                if items:
                    nbA = len(blocks)
                    per = 1
                    pos = [0]

                    def wrap(cf, last):
                        def f(i):
                            cf(i)
                            n = len(items) - pos[0] if last else per
                            for _ in range(n):
                                if pos[0] < len(items):
                                    items[pos[0]]()
                                    pos[0] += 1
                        return f
                    for bj in range(nb0, nbA):
                        blocks[bj]['compute'] = wrap(blocks[bj]['compute'], bj == nbA - 1)
                branch_b()
                out_proj()

            passes = []
            for ti in range(nptiles):
                for l in range(depth):
                    passes.append(('P', ti, l))
            if sample:
                for l in range(depth):
                    passes.append(('S', 0, l))

            def load_x(grp, ti):
                if grp == 'P':
                    for s in range(4):
                        dma('sp', X[:, s, :], xp[ti * 512 + s * 128:ti * 512 + (s + 1) * 128, :], ('X', s), w=[('X', s)])
                else:
                    dma('sp', X[0:16, 0, :], xs[:, :], ('X', 0), w=[('X', 0)])

            seen_l = set()
            for (grp, ti, l) in passes:
                nb0 = len(blocks)
                cur['l'] = l
                cur['first'] = l not in seen_l
                seen_l.add(l)
                layer_pass(grp, ti, l, l == depth - 1)
                if l == 0:
                    cf = blocks[nb0]['compute']
                    blocks[nb0]['compute'] = (lambda i, cf=cf, grp=grp, ti=ti: (load_x(grp, ti), cf(i)))

            def issue_loads(bi):
                i = bi % 2
                blk = blocks[bi]
                l_ = blk['l']
                idx_map = wsc_idx[l_]
                name = blk['name']
                cached = name in idx_map
                if not cached:
                    idx_map[name] = len(idx_map)
                idx = idx_map[name]
                for ld in blk['loads'](i):
                    if len(ld) == 3:
                        dma('pool', ld[0], ld[1], 'Wk', w=['Wk'])
                    elif cached:
                        dma('sp', W[i][:, :], Wsc[l_, idx], ('W', i), r=[('Wsc', l_, idx)], w=[('W', i)])
                    else:
                        dma('pool', ld[0], ld[1], ('W', i), w=[('W', i)])
                        if USE_WSC:
                            dma('sp', Wsc[l_, idx], W[i][:, :], ('Wwb', i), r=[('W', i)], w=[('Wsc', l_, idx)])
                if not cached and not USE_WSC:
                    del idx_map[name]

            issue_loads(0)
            for bi in range(len(blocks)):
                if bi + 1 < len(blocks):
                    issue_loads(bi + 1)
                blocks[bi]['compute'](bi % 2)

            S.finish('sp')
    return nc


_NC_CACHE = {}


def _rope_tables():
    half = 32
    inv = (np.float32(10000.0) ** (-np.arange(half, dtype=np.float32) / np.float32(half))).astype(np.float32)
    pos = np.arange(NPOS, dtype=np.float32)
    ang = (pos[:, None] * inv[None, :]).astype(np.float32)
    c = np.cos(ang).astype(np.float32)
    s = np.sin(ang).astype(np.float32)
    tm = np.concatenate([c, s], axis=1).astype(np.float32)
    fm = np.zeros((2, 64, NPOS), np.float32)
    fm[0, 0:32] = c.T
    fm[0, 32:64] = c.T
    fm[1, 0:32] = -s.T
    fm[1, 32:64] = s.T
    return tm, fm


def kernel(x_prompt, x_sample, cache_ckv, cache_kpe, state_conv,
           norm_gain, w_in, a_v_gain, a_ws, a_bias,
           b_q_gain, b_w_qb, b_kv_gain, b_kr_gain, b_w_kvb,
           b_qn_gain, b_qr_gain, b_kn_gain, c_conv_w,
           w_pa, w_pb, w_pc, w_out, _cfg=None):
    cfg = _cfg or dict(nptiles=4, depth=2, sample=True)
    f = lambda a: np.ascontiguousarray(np.asarray(a, dtype=np.float32))
    qb = f(b_w_qb).reshape(2, 512, 16, 192)
    kvb = f(b_w_kvb).reshape(2, 512, 16, 256)
    w_qkv = np.concatenate([qb[..., 0:128], qb[..., 128:192], qb[..., 160:192], qb[..., 128:160],
                            kvb[..., 0:128], kvb[..., 128:256]], axis=-1).reshape(2, 512, 8192)
    qr = f(b_qr_gain)
    qr2 = np.stack([qr, np.concatenate([qr[:, 32:64], qr[:, 0:32]], axis=1)], axis=1)
    tm, fm = _rope_tables()
    key = (cfg['nptiles'], cfg['depth'], cfg['sample'])
    if key not in _NC_CACHE:
        _NC_CACHE[key] = build(**cfg)
    nc = _NC_CACHE[key]
    def tile_w(w, c0, ncols):
        w = w[:, :, c0:c0 + ncols]
        K = w.shape[1]
        return np.ascontiguousarray(w.reshape(2, K // 128, 128, ncols).transpose(0, 2, 1, 3)).reshape(2, 128, (K // 128) * ncols)
    win = f(w_in)
    w_in_t = np.stack([tile_w(win, c0, 512) for c0 in WIN_BLOCKS], axis=1)
    w_kpe = np.ascontiguousarray(win[:, :, C_KPE:C_KPE + 64])
    tl4 = lambda w, nc_: np.stack([tile_w(w, j * nc_, nc_) for j in range(4)], axis=1)
    shared = dict(norm_gain=f(norm_gain), w_in_t=w_in_t, w_kpe=w_kpe, a_v_gain=f(a_v_gain), a_ws=f(a_ws), a_bias=f(a_bias).reshape(2, 1, 512),
                  b_q_gain=f(b_q_gain), w_qkv_t=tl4(np.ascontiguousarray(w_qkv), 2048), b_kv_gain=f(b_kv_gain), b_kr_gain=f(b_kr_gain),
                  b_qn_gain=f(b_qn_gain), qr2=np.ascontiguousarray(qr2), b_kn_gain=f(b_kn_gain), c_conv_w=f(c_conv_w),
                  w_pa_t=tl4(f(w_pa), 512), w_pb_t=tl4(f(w_pb), 512), w_pc_t=tl4(f(w_pc), 512), w_out_t=tl4(f(w_out), 512),
                  rope_tm=tm, rope_fm=fm)
    xp_ = f(x_prompt); xs_ = f(x_sample); cc = f(cache_ckv); ck = f(cache_kpe); sc = f(state_conv)
    in_maps = []
    for b in range(8):
        m = dict(shared)
        m.update(xp=xp_[b], xs=xs_[b], cckv=np.ascontiguousarray(cc[:, b]), ckpe=np.ascontiguousarray(ck[:, b]),
                 sconv=np.ascontiguousarray(sc[:, b]))
        in_maps.append(m)
    res = run_bass_kernel_spmd(nc, in_maps, core_ids=list(range(8)))
    R = res.results
    st = lambda k, ax: np.stack([np.asarray(R[b][k], dtype=np.float32) for b in range(8)], axis=ax)
    return (st('yp', 0), st('ys', 0), st('o_ckv_p', 1), st('o_kpe_p', 1), st('o_conv_p', 1),
            st('o_ckv_s', 1), st('o_kpe_s', 1), st('o_conv_s', 1), st('o_av_s', 1))
```

```python
import math
import numpy as np
import concourse.bass as bass
import concourse.mybir as mybir
from concourse.bass_utils import run_bass_kernel_spmd

F32 = mybir.dt.float32
BF16 = mybir.dt.bfloat16
AF = mybir.ActivationFunctionType
ALU = mybir.AluOpType

D = 2048
SEQ = 2048
DEPTH = 2
NIN = 16448
EPS = 1e-6
SCALE = 192 ** -0.5
C_AU, C_AV, C_AZ, C_CQ, C_CKV, C_KPE, C_BZ = 0, 1024, 2048, 3072, 3584, 4096, 4160
C_CH, C_CB, C_CC, C_CZ, C_G = 6208, 7232, 8256, 9280, 10304
NPOS = 17 * 128
WIN_BLOCKS = sorted([C_AU, C_AU + 512, C_AV, C_AV + 512, C_AZ, C_AZ + 512, C_CQ, C_CKV] + [C_BZ + b * 512 for b in range(4)]
                    + [c + b * 512 for c in (C_CH, C_CB, C_CC, C_CZ) for b in range(2)] + [C_G + j * 512 for j in range(12)])
WIN_IDX = {c: i for i, c in enumerate(WIN_BLOCKS)}
USE_WSC = True
LA = 2


class Sched:
    def __init__(self, nc):
        self.nc = nc
        self.E = {'pe': nc.tensor, 'act': nc.scalar, 'dve': nc.vector, 'pool': nc.gpsimd, 'sp': nc.sync}
        self.sem = {}
        self.cnt = {}
        self.waited = {e: {} for e in self.E}
        self.buf = {}
        self.dsem = {}
        for e in self.E:
            self.sem[e] = nc.alloc_semaphore('pg_' + e)
            self.cnt[e] = 0

    def _wait(self, E, reads, writes, extra=()):
        need = {}

        def add(st):
            if st is None:
                return
            k, v = st
            if need.get(k, 0) < v:
                need[k] = v
        for r in reads:
            b = self.buf.get(r)
            if b:
                add(b[0])
        for w in writes:
            b = self.buf.get(w)
            if b:
                add(b[0])
                for k, v in b[1].items():
                    add((k, v))
        for st in extra:
            add(st)
        for k, v in need.items():
            if k == 'pe' and E == 'pe':
                continue
            if self.waited[E].get(k, 0) < v:
                self.waited[E][k] = v
                self.E[E].wait_ge(self.sem[k], v)

    def _mark(self, st, reads, writes):
        k, v = st
        for r in reads:
            b = self.buf.setdefault(r, [None, {}])
            if b[1].get(k, 0) < v:
                b[1][k] = v
        for w in writes:
            self.buf[w] = [st, {}]

    def op(self, E, fn, r=(), w=(), inc=True):
        self._wait(E, r, w)
        ins = fn(self.E[E])
        if inc:
            self.cnt[E] += 1
            ins.then_inc(self.sem[E], 1)
            st = (E, self.cnt[E])
        else:
            st = (E, self.cnt[E] + 1)
        self._mark(st, r, w)

    def dma(self, Q, out, in_, key, r=(), w=(), nc_ok=False):
        if key not in self.dsem:
            name = 'd%d' % len(self.dsem)
            self.sem[name] = self.nc.alloc_semaphore(name)
            self.dsem[key] = [name, 0]
        ds = self.dsem[key]
        extra = [(ds[0], ds[1])] if ds[1] > 0 else []
        self._wait(Q, r, w, extra)
        ds[1] += 16
        if nc_ok:
            ins = self.E[Q].dma_start(out=out, in_=in_, allow_slow_non_contiguous=True)
        else:
            ins = self.E[Q].dma_start(out=out, in_=in_)
        ins.then_inc(self.sem[ds[0]], 16)
        self._mark((ds[0], ds[1]), r, w)

    def finish(self, Q):
        for key, ds in self.dsem.items():
            if ds[1] > 0 and self.waited[Q].get(ds[0], 0) < ds[1]:
                self.waited[Q][ds[0]] = ds[1]
                self.E[Q].wait_ge(self.sem[ds[0]], ds[1])


def build(nptiles=4, depth=DEPTH, sample=True):
    nc = bass.Bass("TRN2", target_bir_lowering=False)

    def din(name, shape):
        return nc.dram_tensor(name, list(shape), F32, kind="ExternalInput").ap()

    def dout(name, shape):
        return nc.dram_tensor(name, list(shape), F32, kind="ExternalOutput").ap()

    xp = din("xp", [SEQ, D]); xs = din("xs", [16, D])
    cckv = din("cckv", [2, 2048, 512]); ckpe = din("ckpe", [2, 2048, 64]); sconv = din("sconv", [2, 2, 1024])
    norm_gain = din("norm_gain", [2, D]); w_in_t = din("w_in_t", [2, len(WIN_BLOCKS), 128, 8192]); w_kpe = din("w_kpe", [2, D, 64])
    a_v_gain = din("a_v_gain", [2, 1024]); a_ws = din("a_ws", [2, 4, 128, 128]); a_bias = din("a_bias", [2, 1, 512])
    b_q_gain = din("b_q_gain", [2, 512]); w_qkv_t = din("w_qkv_t", [2, 4, 128, 8192])
    b_kv_gain = din("b_kv_gain", [2, 512]); b_kr_gain = din("b_kr_gain", [2, 64])
    b_qn_gain = din("b_qn_gain", [2, 128]); qr2 = din("qr2", [2, 2, 64]); b_kn_gain = din("b_kn_gain", [2, 128])
    c_conv_w = din("c_conv_w", [2, 3, 1024])
    w_pa = din("w_pa_t", [2, 4, 128, 4096]); w_pb = din("w_pb_t", [2, 4, 128, 8192]); w_pc = din("w_pc_t", [2, 4, 128, 4096])
    w_out = din("w_out_t", [2, 4, 128, 8192])
    rope_tm = din("rope_tm", [NPOS, 64]); rope_fm = din("rope_fm", [2, 64, NPOS])

    yp = dout("yp", [SEQ, D]); ys = dout("ys", [16, D])
    o_ckv_p = dout("o_ckv_p", [2, SEQ, 512]); o_kpe_p = dout("o_kpe_p", [2, SEQ, 64]); o_conv_p = dout("o_conv_p", [2, 2, 1024])
    o_ckv_s = dout("o_ckv_s", [2, 16, 512]); o_kpe_s = dout("o_kpe_s", [2, 16, 64]); o_conv_s = dout("o_conv_s", [2, 2, 1024])
    o_av_s = dout("o_av_s", [2, 16, 1024])

    Ksc = nc.dram_tensor("Ksc", [2, 2, 16, 128, 2048], BF16).ap()
    Vsc = nc.dram_tensor("Vsc", [2, 2, 4, 16, 128, 512], BF16).ap()
    Wsc = nc.dram_tensor("Wsc", [2, 56, 128, 8192], BF16).ap()

    S = Sched(nc)
    pools = []

    def sb(name, shape, dt):
        g = nc.sbuf_tensor(name, list(shape), dt)
        pools.append(g)
        return g.__enter__()

    def pstile(name, shape, dt):
        g = nc.psum_tensor(name, list(shape), dt)
        pools.append(g)
        return g.__enter__()

    X = sb("X", [128, 4, D], F32)
    big = sb("big", [128, 8192], BF16)
    xnT = sb("xnT", [128, 16, 512], BF16)
    W = [sb("W0", [128, 8192], BF16), sb("W1", [128, 8192], BF16), X[:, 1:3, :].rearrange("p a b -> p (a b)").bitcast(BF16)]
    Wk = sb("Wk", [128, 16, 64], BF16)
    yb = sb("yb", [128, 16, 512], BF16)
    yac = sb("yac", [128, 8, 512], BF16)
    hv = sb("hv", [128, 8 * 514], BF16)
    cqT = sb("cqT", [128, 4, 512], BF16)
    ckT = sb("ckT", [128, 4, 512], BF16)
    kpT = sb("kpT", [128, 2, 2064], BF16)
    qns = [sb("qn0", [128, 512], BF16), sb("qn1", [128, 512], BF16)]; qpes = [sb("qpe0", [128, 512], BF16), sb("qpe1", [128, 512], BF16)]; kcur = sb("kcur", [128, 512], BF16)
    Vg = sb("Vg", [128, 4, 512], BF16)
    KVb = sb("KVb", [128, 6144], BF16)
    NPT = 4
    PT = [sb("PT%d" % i, [128, 512], BF16) for i in range(NPT)]
    sq = sb("sq", [128, 512], BF16)
    rs = sb("rs", [128, 512], F32)
    sqs = [sq, sb("sq1", [128, 512], BF16)]
    rss = [rs, sb("rs1", [128, 512], F32)]
    sqk = ['sq', 'sq1']
    rsk = ['rs', 'rs1']
    kcurs = [kcur, sb("kcur1", [128, 512], BF16), sb("kcur2", [128, 512], BF16)]
    kck = ['kcur', 'kcur1', 'kcur2']
    rot = {'sq': 0, 'rs': 0, 'kcur': 0, 'pts': 0}

    def nxt(name, n):
        i = rot[name] % n
        rot[name] += 1
        return i
    tf = [sb("tf%d" % i, [128, 512], F32) for i in range(2)]
    sg = sb("sg", [128, 4, 512], BF16)
    iot = tf[0][:, 0:128]
    wsraw = tf[1][:, 0:128]
    wsb = sq[:, 0:128]
    trilm = sqs[1][:, 0:128]
    bias32 = tf[1][0:1, 0:512]

    gCS = sb("gCS", [128, 2, 512], F32)
    kst1 = sb("kst1", [128, 4, 64], F32)
    kpb = sb("kpb", [128, 4, 64], BF16)
    sm = sb("sm", [128, 64], F32)
    cst = sb("cst", [128, 2, 8, 2], BF16)
    cst32 = sb("cst32", [128, 8, 2], F32)
    ident = sb("ident", [128, 128], BF16)
    ones = sb("ones", [128, 128], BF16)
    mean128 = sb("mean128", [128, 128], BF16)
    mean64 = sb("mean64", [128, 64], BF16)
    epsc = sb("epsc", [128, 1], F32)
    ngc = sb("ngc", [128, 2, 16], F32)
    avg_bc = sb("avg_bc", [128, 1024], F32)
    qg_bc = sb("qg_bc", [128, 512], F32)
    kvg_bc = sb("kvg_bc", [128, 512], F32)
    krg_bc = sb("krg_bc", [128, 2, 64], F32)
    wsmT = sb("wsmT", [128, 2, 4, 128], BF16)
    biash = sb("biash", [1, 512], BF16)
    biasl = sb("biasl", [1, 512], BF16)
    qngc = sb("qngc", [128, 2], F32); kngc = sb("kngc", [128, 2], F32)
    qr2c = sb("qr2c", [128, 2, 2], F32)
    cwc = sb("cwc", [128, 2, 3, 8], F32)
    ropTM = sb("ropTM", [128, 4, 64], F32)

    print('sbuf bytes remaining', nc.sbuf_bytes_remaining)
    ps = [pstile("ps%d" % i, [128, 512], F32) for i in range(7)]
    psT = pstile("psT", [128, 8, 128], BF16)
    GEN = [0, 1, 2, 5, 6]
    gen_i = [0]

    def gbank():
        i = GEN[gen_i[0] % len(GEN)]
        gen_i[0] += 1
        return i

    op = S.op
    dma = S.dma

    with nc.Block() as block:
        @block.sync
        def _(sync_eng):
            op('pool', lambda e: e.iota(iot, [[1, 128]], base=0, channel_multiplier=-1,
                                        allow_small_or_imprecise_dtypes=True), w=[('tf', 0)])
            op('dve', lambda e: e.tensor_single_scalar(out=ident[:], in_=iot, scalar=0.0, op=ALU.is_equal), r=[('tf', 0)], w=['ident'])
            op('dve', lambda e: e.tensor_single_scalar(out=trilm, in_=iot, scalar=0.0, op=ALU.is_ge), r=[('tf', 0)], w=['sq1'])
            op('dve', lambda e: e.memset(ones[:], 1.0), w=['ones'])
            op('dve', lambda e: e.memset(mean128[:], 1.0 / 128), w=['mean128'])
            op('dve', lambda e: e.memset(mean64[:], 1.0 / 64), w=['mean64'])
            op('dve', lambda e: e.memset(epsc[:], EPS), w=['epsc'])
            dma('sp', ngc[:], norm_gain.rearrange("l (kc p) -> p l kc", p=128), 'ngc', w=['ngc'], nc_ok=True)
            dma('sp', krg_bc[:].rearrange("p l j -> p (l j)"), b_kr_gain.rearrange("l j -> (l j)").partition_broadcast(128), 'krg', w=['krg'])
            dma('sp', qngc[:], b_qn_gain.rearrange("l p -> p l"), 'qngc', w=['qngc'], nc_ok=True)
            dma('sp', kngc[:], b_kn_gain.rearrange("l p -> p l"), 'kngc', w=['kngc'], nc_ok=True)
            dma('sp', qr2c[0:64], qr2.rearrange("l s j -> j l s"), 'qr2c', w=['qr2c'], nc_ok=True)
            for l_ in range(2):
                dma('sp', cwc[:, l_], c_conv_w[l_].rearrange("k (c p) -> p k c", p=128), 'cwc', w=['cwc'], nc_ok=True)
            for l in range(2):
                for g in range(4):
                    dma('sp', wsraw, a_ws[l, g], ('tf', 1), w=[('tf', 1)])
                    op('dve', lambda e: e.tensor_copy(out=wsb, in_=wsraw), r=[('tf', 1)], w=['sq'])
                    op('pe', lambda e: e.transpose(out=psT[:, 0, :], in_=wsb, identity=ident[:]), r=['sq', 'ident'], w=['psT'])
                    op('dve', lambda e, l=l, g=g: e.tensor_tensor(out=wsmT[:, l, g, :], in0=psT[:, 0, :], in1=trilm, op=ALU.mult),
                       r=['psT', 'sq1'], w=['wsmT'])

            blocks = []

            def win_view(l, c0, ncol):
                return w_in[l].rearrange("(kc p) n -> p kc n", p=128)[:, :, c0:c0 + ncol]

            def wslot3(i, k, n):
                return W[i][:, 0:k * n].rearrange("p (k n) -> p k n", n=n)

            cur = {'l': 0, 'first': True}
            wsc_idx = [dict(), dict()]

            def add_block(loads_fn, compute_fn, name):
                blocks.append({'loads': loads_fn, 'compute': compute_fn, 'name': name, 'l': cur['l'], 'first': cur['first']})

            def rstd_small(col_in, col_out, P, n, inv_n, key='sm'):
                op('act', lambda e: e.activation(out=sm[0:P, col_out:col_out + n], in_=sm[0:P, col_in:col_in + n], func=AF.Ln,
                                                 scale=inv_n, bias=epsc[0:P, 0:1]), r=[key, 'epsc'], w=[key])
                op('act', lambda e: e.activation(out=sm[0:P, col_out:col_out + n], in_=sm[0:P, col_out:col_out + n], func=AF.Exp,
                                                 scale=-0.5), r=[key], w=[key])

            def fm_group(bank, lhs_fn, rhs_fn, K, M, T, rkeys, inc_last=True):
                for kc in range(K):
                    last = kc == K - 1
                    op('pe', lambda e, kc=kc, last=last: e.matmul(ps[bank][0:M, 0:T], lhsT=lhs_fn(kc), rhs=rhs_fn(kc),
                                                                    start=(kc == 0), stop=last),
                       r=rkeys, w=[('ps', bank)], inc=last and inc_last)

            def fm_norm(bank, M, T, meanm, gcol, out_ap, out_key):
                si = nxt('sq', 2)
                ri = nxt('rs', 2)
                sq_, rs_ = sqs[si], rss[ri]
                op('act', lambda e: e.activation(out=sq_[0:M, 0:T], in_=ps[bank][0:M, 0:T], func=AF.Square), r=[('ps', bank)], w=[sqk[si]])
                sb_ = gbank()
                op('pe', lambda e: e.matmul(ps[sb_][0:M, 0:T], lhsT=meanm, rhs=sq_[0:M, 0:T], start=True, stop=True),
                   r=[sqk[si], 'mean128', 'mean64'], w=[('ps', sb_)])
                op('act', lambda e: e.activation(out=rs_[0:M, 0:T], in_=ps[sb_][0:M, 0:T], func=AF.Ln, bias=epsc[0:M, 0:1]),
                   r=[('ps', sb_), 'epsc'], w=[rsk[ri]])
                op('act', lambda e: e.activation(out=rs_[0:M, 0:T], in_=rs_[0:M, 0:T], func=AF.Exp, scale=-0.5), r=[rsk[ri]], w=[rsk[ri]])
                if out_ap is not None:
                    op('dve', lambda e: e.scalar_tensor_tensor(out=out_ap, in0=ps[bank][0:M, 0:T], scalar=gcol, in1=rs_[0:M, 0:T],
                                                               op0=ALU.mult, op1=ALU.mult),
                       r=[('ps', bank), rsk[ri], 'qngc', 'kngc'], w=[out_key])
                return ri

            def layer_pass(grp, ti, l, last_layer):
                P = grp == 'P'
                T = 512 if P else 16
                NS = 4 if P else 1
                TS = 128 if P else 16
                gi = 0 if P else 1
                base = ti * 512 if P else 2048
                sidx0 = ti * 4 if P else 16
                nprev = ti if P else 4

                def tok(s):
                    return slice(s * 128, s * 128 + TS)

                def xsb(s):
                    return big[0:TS, s * 2048:(s + 1) * 2048]

                def hT(kc):
                    return big[:, kc * 512:(kc + 1) * 512]

                def hcv(c):
                    return hv[:, c * 514:(c + 1) * 514]

                def vnv(s):
                    return hv[:, s * 1024:(s + 1) * 1024]

                first_merge = [True]
                has_partial = [P and l > 0]

                def phase_norm():
                    for s in range(NS):
                        key = ('smn', s)
                        if has_partial[0]:
                            op('dve', lambda e, s=s: e.reduce_sum(out=sm[0:TS, s:s + 1], in_=sm[0:TS, 24 + s * 4:28 + s * 4], axis=mybir.AxisListType.X),
                               r=[key], w=[key])
                        else:
                            op('act', lambda e, s=s: e.activation(out=xsb(s), in_=X[0:TS, s, :], func=AF.Square, accum_out=sm[0:TS, s:s + 1]),
                               r=[('X', s)], w=['big', key])
                        rstd_small(s, 4 + s, TS, 1, 1.0 / D, key)
                        if s % 2 == 0:
                            op('act', lambda e, s=s: e.activation(out=xsb(s), in_=X[0:TS, s, :], func=AF.Copy, scale=sm[0:TS, 4 + s:5 + s]),
                               r=[('X', s), key], w=['big'])
                        else:
                            op('dve', lambda e, s=s: e.tensor_scalar(out=xsb(s), in0=X[0:TS, s, :], scalar1=sm[0:TS, 4 + s:5 + s], scalar2=1.0,
                                                                     op0=ALU.mult, op1=ALU.mult),
                               r=[('X', s), key], w=['big'])
                    for kc in range(16):
                        for s in range(NS):
                            op('pe', lambda e, s=s, kc=kc: e.transpose(out=psT[:, s, 0:TS], in_=xsb(s)[:, kc * 128:(kc + 1) * 128],
                                                                        identity=ident[0:TS, 0:TS]),
                               r=['big', 'ident'], w=['psT'], inc=(s == NS - 1))
                        op('dve', lambda e, kc=kc: e.tensor_scalar(out=xnT[:, kc, 0:NS * 128].rearrange("p (s t) -> p s t", t=128)[:, :, 0:TS],
                                                                    in0=psT[:, 0:NS, 0:TS], scalar1=ngc[:, l, kc:kc + 1], scalar2=1.0, op0=ALU.mult, op1=ALU.mult),
                           r=['psT', 'ngc'], w=['xnT'])

                def xn_rhs(kc):
                    return xnT[:, kc, 0:T] if P else xnT[:, kc, 0:16]

                def xn_lhs(kc, s):
                    return xnT[:, kc, tok(s)]

                def fm_win_block(l, c0, evac):
                    def loads(i):
                        return [(W[i][:, :], w_in_t[l, WIN_IDX[c0]])]

                    def compute(i):
                        wv = wslot3(i, 16, 512)
                        for j in range(4):
                            bk = gbank()
                            fm_group(bk, lambda kc, j=j: wv[:, kc, j * 128:(j + 1) * 128], xn_rhs, 16, 128, T, [('W', i), 'xnT'])
                            evac(j, bk)
                    add_block(loads, compute, ('win', c0))

                def tm_win_block(l, c0, evac, extra_loads=None, post=None):
                    def loads(i):
                        ls = [(W[i][:, :], w_in_t[l, WIN_IDX[c0]])]
                        return ls

                    def compute(i):
                        wv = wslot3(i, 16, 512)
                        for s in range(NS):
                            bk = gbank()
                            fm_group(bk, lambda kc, s=s: xn_lhs(kc, s), lambda kc: wv[:, kc, :], 16, TS, 512, [('W', i), 'xnT'])
                            evac(s, bk)
                        if post:
                            post()
                    add_block(loads, compute, ('win', c0))

                def merge_branch(k, wp, Kc, yfn, ykeys):
                    fm = first_merge[0]
                    first_merge[0] = False
                    for dmg in range(4):
                        def ev_gate(j, bk):
                            op('act', lambda e: e.activation(out=sg[:, j, 0:T], in_=ps[bk][:, 0:T], func=AF.Sigmoid),
                               r=[('ps', bk)], w=[('sg', j)])
                        fm_win_block(l, C_G + k * 2048 + dmg * 512, ev_gate)

                        def loads(i, dmg=dmg):
                            return [(W[i][:, 0:Kc * 512], wp[l, dmg])]

                        def compute(i, dmg=dmg):
                            wv = wslot3(i, Kc, 512)
                            for j in range(4):
                                n = dmg * 4 + j
                                bk = gbank()
                                fm_group(bk, lambda kc: wv[:, kc, j * 128:(j + 1) * 128], lambda kc: yfn(kc), Kc, 128, T, [('W', i)] + ykeys)
                                if fm:
                                    op('dve', lambda e: e.tensor_tensor(out=hT(n)[:, 0:T], in0=ps[bk][:, 0:T], in1=sg[:, j, 0:T], op=ALU.mult),
                                       r=[('ps', bk), ('sg', j)], w=['big'])
                                else:
                                    t = tf[n % 2]
                                    op('dve', lambda e: e.tensor_tensor(out=t[:, 0:T], in0=ps[bk][:, 0:T], in1=sg[:, j, 0:T], op=ALU.mult),
                                       r=[('ps', bk), ('sg', j)], w=[('tf', n % 2)])
                                    op('dve', lambda e: e.tensor_tensor(out=hT(n)[:, 0:T], in0=hT(n)[:, 0:T], in1=t[:, 0:T], op=ALU.add),
                                       r=[('tf', n % 2), 'big'], w=['big'])
                        add_block(loads, compute, ('p', k, dmg))

                def branch_c():
                    def pre():
                        hc3 = hv[:, :].rearrange("p (c t) -> p c t", t=514)
                        if P and ti == 0:
                            op('dve', lambda e: e.memset(hc3[:, :, 0:2], 0.0), w=['hv'])
                        elif P:
                            op('dve', lambda e: e.tensor_copy(out=hc3[:, :, 0:2], in_=cst[:, l, :, :]), r=['cst'], w=['hv'])
                        else:
                            for t_ in range(2):
                                dma('sp', cst32[:, :, t_], sconv[l, t_].rearrange("(c p) -> p c", p=128), 'cst32', w=['cst32'], nc_ok=True)
                            op('dve', lambda e: e.tensor_copy(out=hc3[:, :, 0:2], in_=cst32[:]), r=['cst32'], w=['hv'])
                    for b in range(2):
                        def ev_ch(j, bk, b=b):
                            c = b * 4 + j
                            op('act', lambda e: e.activation(out=yac[:, c, 0:T], in_=ps[bk][:, 0:T], func=AF.Copy), r=[('ps', bk)], w=[('yac', c)])
                        fm_win_block(l, C_CH + b * 512, ev_ch)
                    for b in range(2):
                        def ev_cc(j, bk, b=b):
                            c = b * 4 + j
                            if c == 0:
                                pre()
                            op('dve', lambda e: e.tensor_tensor(out=hcv(c)[:, 2:2 + T], in0=ps[bk][:, 0:T], in1=yac[:, c, 0:T], op=ALU.mult),
                               r=[('ps', bk), ('yac', c)], w=['hv'])
                            t = tf[c % 2]
                            op('dve', lambda e: e.tensor_scalar(out=t[:, 0:T], in0=hcv(c)[:, 2:2 + T], scalar1=cwc[:, l, 2, c:c + 1], scalar2=1.0, op0=ALU.mult, op1=ALU.mult),
                               r=['hv', 'cwc'], w=[('tf', c % 2)])
                            op('dve', lambda e: e.scalar_tensor_tensor(out=t[:, 0:T], in0=hcv(c)[:, 1:1 + T], scalar=cwc[:, l, 1, c:c + 1], in1=t[:, 0:T],
                                                                       op0=ALU.mult, op1=ALU.add), r=['hv', 'cwc', ('tf', c % 2)], w=[('tf', c % 2)])
                            op('dve', lambda e: e.scalar_tensor_tensor(out=yac[:, c, 0:T], in0=hcv(c)[:, 0:T], scalar=cwc[:, l, 0, c:c + 1], in1=t[:, 0:T],
                                                                       op0=ALU.mult, op1=ALU.add), r=['hv', 'cwc', ('tf', c % 2)], w=[('yac', c)])
                            if c == 7:
                                hc3 = hv[:, :].rearrange("p (c t) -> p c t", t=514)
                                if P:
                                    op('dve', lambda e: e.tensor_copy(out=cst[:, l, :, :], in_=hc3[:, :, T:T + 2]), r=['hv'], w=['cst'])
                                if (P and ti == nptiles - 1 and nptiles == 4) or not P:
                                    op('dve', lambda e: e.tensor_copy(out=cst32[:], in_=hc3[:, :, T:T + 2]), r=['hv'], w=['cst32'])
                                    for t_ in range(2):
                                        dst = (o_conv_p if P else o_conv_s)[l, t_].rearrange("(c p) -> p c", p=128)
                                        dma('sp', dst, cst32[:, :, t_], 'cst32', r=['cst32'], nc_ok=True)
                        fm_win_block(l, C_CC + b * 512, ev_cc)
                    for b in range(2):
                        def ev_cz(j, bk, b=b):
                            c = b * 4 + j
                            op('act', lambda e: e.activation(out=sq[:, 0:T], in_=ps[bk][:, 0:T], func=AF.Silu), r=[('ps', bk)], w=['sq'])
                            op('dve', lambda e: e.tensor_tensor(out=yac[:, c, 0:T], in0=yac[:, c, 0:T], in1=sq[:, 0:T], op=ALU.mult),
                               r=['sq', ('yac', c)], w=[('yac', c)])
                        fm_win_block(l, C_CZ + b * 512, ev_cz)
                    for b in range(2):
                        def ev_cb(j, bk, b=b):
                            c = b * 4 + j
                            op('dve', lambda e: e.tensor_tensor(out=yac[:, c, 0:T], in0=ps[bk][:, 0:T], in1=yac[:, c, 0:T], op=ALU.mult),
                               r=[('ps', bk), ('yac', c)], w=[('yac', c)])
                        fm_win_block(l, C_CB + b * 512, ev_cb)
                    merge_branch(2, w_pc, 8, lambda kc: yac[:, kc, 0:T], [('yac', c) for c in range(8)])

                def branch_a():
                    for b in range(2):
                        def ev_az(j, bk, b=b):
                            c = b * 4 + j
                            op('act', lambda e: e.activation(out=yac[:, c, 0:T], in_=ps[bk][:, 0:T], func=AF.Silu), r=[('ps', bk)], w=[('yac', c)])
                        fm_win_block(l, C_AZ + b * 512, ev_az)
                    for b in range(2):
                        def ev_au(j, bk, b=b):
                            c = b * 4 + j
                            op('dve', lambda e: e.tensor_tensor(out=yac[:, c, 0:T], in0=ps[bk][:, 0:T], in1=yac[:, c, 0:T], op=ALU.mult),
                               r=[('ps', bk), ('yac', c)], w=[('yac', c)])
                        fm_win_block(l, C_AU + b * 512, ev_au)

                    def mixing():
                        for c in range(8):
                            g = c // 2
                            bk = gbank()
                            for s in range(NS):
                                o_ = ps[bk][:, tok(s)]
                                op('pe', lambda e, s=s: e.matmul(o_, lhsT=vnv(s)[0:TS, c * 128:(c + 1) * 128], rhs=wsmT[0:TS, l, g, 0:TS], start=True, stop=False),
                                   r=['hv', 'wsmT'], w=[('ps', bk)], inc=False)
                                op('pe', lambda e: e.matmul(o_, lhsT=ones[0:1, :], rhs=biash[0:1, g * 128:g * 128 + TS], start=False, stop=False),
                                   r=['ones', 'biash'], w=[('ps', bk)], inc=False)
                                op('pe', lambda e: e.matmul(o_, lhsT=ones[0:1, :], rhs=biasl[0:1, g * 128:g * 128 + TS], start=False, stop=True),
                                   r=['ones', 'biasl'], w=[('ps', bk)], inc=(s == NS - 1))
                            op('dve', lambda e: e.tensor_tensor(out=yac[:, c, 0:T], in0=ps[bk][:, 0:T], in1=yac[:, c, 0:T], op=ALU.mult),
                               r=[('ps', bk), ('yac', c)], w=[('yac', c)])

                    for b in range(2):
                        def ev_av(s, bk, b=b):
                            for gl in range(2):
                                op('act', lambda e, gl=gl: e.activation(out=sq[0:TS, 0:256], in_=ps[bk][0:TS, gl * 256:(gl + 1) * 256], func=AF.Square,
                                                                       accum_out=sm[0:TS, 8 + gl:9 + gl]), r=[('ps', bk)], w=['sq', 'sma'])
                            rstd_small(8, 10, TS, 2, 1.0 / 256, 'sma')
                            for gl in range(2):
                                cs = slice(b * 512 + gl * 256, b * 512 + (gl + 1) * 256)
                                if P:
                                    op('dve', lambda e, gl=gl, cs=cs: e.scalar_tensor_tensor(out=vnv(s)[0:TS, cs], in0=ps[bk][0:TS, gl * 256:(gl + 1) * 256],
                                                                                         scalar=sm[0:TS, 10 + gl:11 + gl], in1=avg_bc[0:TS, cs], op0=ALU.mult, op1=ALU.mult),
                                       r=[('ps', bk), 'sma', 'avg'], w=['hv'])
                                else:
                                    t = tf[gl]
                                    op('dve', lambda e, gl=gl, cs=cs: e.scalar_tensor_tensor(out=t[0:TS, 0:256], in0=ps[bk][0:TS, gl * 256:(gl + 1) * 256],
                                                                                         scalar=sm[0:TS, 10 + gl:11 + gl], in1=avg_bc[0:TS, cs], op0=ALU.mult, op1=ALU.mult),
                                       r=[('ps', bk), 'sma', 'avg'], w=[('tf', gl)])
                                    op('act', lambda e, cs=cs: e.activation(out=vnv(s)[0:TS, cs], in_=t[0:TS, 0:256], func=AF.Copy), r=[('tf', gl)], w=['hv'])
                                    dma('sp', o_av_s[l][:, cs], t[0:TS, 0:256], ('tf', gl), r=[('tf', gl)])
                        tm_win_block(l, C_AV + b * 512, ev_av, post=(mixing if b == 1 else None))
                    merge_branch(0, w_pa, 8, lambda kc: yac[:, kc, 0:T], [('yac', c) for c in range(8)])

                def kv_expand(i, g, ckT_cols, TT, NS_, TS_, tile_idx, grp_i, store, ck=None, ckk='ckT', q='sp', vout=None):
                    ck = ckT if ck is None else ck
                    wq = W[i][:, :].rearrange("p (k h c) -> p k h c", k=4, h=4)
                    for s in range(NS_):
                        bk = gbank()
                        fm_group(bk, lambda kc, s=s: ck[:, kc, s * 128:s * 128 + TS_], lambda kc: wq[:, kc, :, 384:512], 4, TS_, 512, [('W', i), ckk])
                        if vout is not None:
                            op('act', lambda e, s=s: e.activation(out=vout[0][0:TS_, s, :], in_=ps[bk][0:TS_, :], func=AF.Copy), r=[('ps', bk)], w=[vout[1][s]])
                        else:
                            op('act', lambda e, s=s: e.activation(out=Vg[0:TS_, s, :], in_=ps[bk][0:TS_, :], func=AF.Copy), r=[('ps', bk)], w=['Vg'])
                    if store:
                        dst = Vsc[grp_i, l, g, tile_idx * 4:(tile_idx + 1) * 4].rearrange("s p c -> p s c")
                        dma(q, dst, Vg[:, :, :], 'Vg', r=['Vg'], w=[('Vsc', grp_i, l, g)])

                def k_head(i, g, hl, TT, tile_idx, grp_i, store, ck=None, ckk='ckT', q='sp', out=None):
                    ck = ckT if ck is None else ck
                    wq = W[i][:, :].rearrange("p (k h c) -> p k h c", k=4, h=4)
                    h = g * 4 + hl
                    bk = gbank()
                    fm_group(bk, lambda kc: wq[:, kc, hl, 256:384], lambda kc: ck[:, kc, 0:TT], 4, 128, TT, [('W', i), ckk])
                    if out is not None:
                        fm_norm(bk, 128, TT, mean128[:, :], kngc[:, l:l + 1], out[0], out[1])
                        return None
                    ki = nxt('kcur', 3)
                    fm_norm(bk, 128, TT, mean128[:, :], kngc[:, l:l + 1], kcurs[ki][:, 0:TT], kck[ki])
                    if store:
                        dma(q, Ksc[grp_i, l, h, :, tile_idx * 512:(tile_idx + 1) * 512], kcurs[ki][:, 0:TT], kck[ki], r=[kck[ki]], w=[('Ksc', grp_i, l, h)])
                    return ki

                def cexp_items():
                    items = []
                    XB3 = X[:, 3, :].bitcast(BF16)
                    stg = XB3[:, 0:2048].rearrange("p (s c) -> p s c", c=512)
                    ckbufs = [(XB3[:, 2048:4096].rearrange("p (k t) -> p k t", k=4), ('X3', 1)), (ckT, 'ckT')]
                    ucount = [0]

                    def load_w2(g):
                        def f():
                            idx = wsc_idx[l][('qkv', g)]
                            dma('sp', W[2][:, :], Wsc[l, idx], ('W', 2), r=[('Wsc', l, idx)], w=[('W', 2)])
                        return f

                    pending = []

                    def flush():
                        for fn in pending:
                            fn()
                        pending[:] = []

                    def unit(g, c4):
                        def f():
                            u = ucount[0]
                            ck, ckk = ckbufs[u % 2]
                            ucount[0] += 1
                            K4 = yb[:, (u % 2) * 4:(u % 2) * 4 + 4, :]
                            K4k = [('yb', (u % 2) * 4 + j) for j in range(4)]
                            V4 = yb[:, 8 + (u % 2) * 4:12 + (u % 2) * 4, :]
                            V4k = [('yb', 8 + (u % 2) * 4 + j) for j in range(4)]
                            dma('pool', stg, cckv[l, c4 * 512:(c4 + 1) * 512, :].rearrange("(s p) c -> p s c", p=128), 'x3stg', w=[('X3', 0)])
                            flush()
                            for kc in range(4):
                                for s in range(4):
                                    op('pe', lambda e, s=s, kc=kc: e.transpose(out=psT[:, s, :], in_=stg[:, s, kc * 128:(kc + 1) * 128], identity=ident[:, :]),
                                       r=[('X3', 0), 'ident'], w=['psT'], inc=(s == 3))
                                op('dve', lambda e, kc=kc: e.tensor_copy(out=ck[:, kc, :].rearrange("p (s t) -> p s t", t=128), in_=psT[:, 0:4, :]),
                                   r=['psT'], w=[ckk])
                            if g == 0:
                                dma('pool', kpb[:, :, :], ckpe[l, c4 * 512:(c4 + 1) * 512, :].rearrange("(s p) c -> p s c", p=128), 'kpbstg', w=['kpb'])
                                for s in range(4):
                                    op('pe', lambda e, s=s: e.transpose(out=psT[0:64, s, :], in_=kpb[:, s, :], identity=ident[:, :]),
                                       r=['kpb', 'ident'], w=['psT'], inc=(s == 3))
                                op('dve', lambda e: e.tensor_copy(out=kpT[0:64, l, c4 * 512:(c4 + 1) * 512].rearrange("p (s t) -> p s t", t=128),
                                                                  in_=psT[0:64, 0:4, :]), r=['psT'], w=['kpT'])
                            kv_expand(2, g, None, 512, 4, 128, c4, 1, False, ck=ck, ckk=ckk, vout=(V4, V4k))
                            for hl in range(4):
                                k_head(2, g, hl, 512, c4, 1, False, ck=ck, ckk=ckk, out=(K4[:, hl, :], K4k[hl]))

                            def stores(u=u, g=g, c4=c4, K4=K4, K4k=K4k, V4=V4, V4k=V4k):
                                dma('pool', Vsc[1, l, g, c4 * 4:(c4 + 1) * 4].rearrange("s p c -> p s c"), V4, ('ybV', u % 2), r=V4k, w=[('Vsc', 1, l, g)])
                                dma('pool', Ksc[1, l, g * 4:(g + 1) * 4, :, c4 * 512:(c4 + 1) * 512].rearrange("h d t -> d h t"), K4, ('ybK', u % 2), r=K4k,
                                    w=[('Ksc', 1, l, g * 4 + j) for j in range(4)])
                            pending.append(stores)
                        return f
                    for g in range(4):
                        items.append(load_w2(g))
                        for c4 in range(4):
                            items.append(unit(g, c4))
                    items.append(flush)
                    return items

                def branch_b():
                    def qkv_loads(g):
                        def loads(i):
                            return [(W[i][:, :], w_qkv_t[l, g])]
                        return loads

                    def lat_block(which, wv, gain_bc, gkey, dstT, dkey, out_fn):
                        banks = []
                        col0 = 44 if which == 'q' else 52

                        def mm(s):
                            bk = gbank()
                            banks.append(bk)
                            fm_group(bk, lambda kc, s=s: xn_lhs(kc, s), lambda kc: wv[:, kc, :], 16, TS, 512, ['xnT'] + wkeys_cur)

                        def chain(s):
                            bk = banks[s]
                            si = s % 2
                            stg = sqs[si]
                            col = col0 + 2 * s
                            skey = ('sml', which, s)
                            op('act', lambda e: e.activation(out=stg[0:TS, :], in_=ps[bk][0:TS, :], func=AF.Square, accum_out=sm[0:TS, col:col + 1]),
                               r=[('ps', bk)], w=[sqk[si], skey])
                            rstd_small(col, col + 1, TS, 1, 1.0 / 512, skey)
                            if out_fn is None:
                                op('dve', lambda e: e.scalar_tensor_tensor(out=stg[0:TS, :], in0=ps[bk][0:TS, :], scalar=sm[0:TS, col + 1:col + 2], in1=gain_bc[0:TS, :],
                                                                           op0=ALU.mult, op1=ALU.mult), r=[('ps', bk), skey, gkey], w=[sqk[si]])
                            else:
                                t = tf[s % 2]
                                op('dve', lambda e: e.scalar_tensor_tensor(out=t[0:TS, :], in0=ps[bk][0:TS, :], scalar=sm[0:TS, col + 1:col + 2], in1=gain_bc[0:TS, :],
                                                                           op0=ALU.mult, op1=ALU.mult), r=[('ps', bk), skey, gkey], w=[('tf', s % 2)])
                                out_fn(s, t)
                                op('act', lambda e: e.activation(out=stg[0:TS, :], in_=t[0:TS, :], func=AF.Copy), r=[('tf', s % 2)], w=[sqk[si]])

                        def tr(s):
                            si = s % 2
                            stg = sqs[si]
                            for c in range(4):
                                op('pe', lambda e, c=c: e.transpose(out=psT[:, c, 0:TS], in_=stg[0:TS, c * 128:(c + 1) * 128], identity=ident[0:TS, 0:TS]),
                                   r=[sqk[si], 'ident'], w=['psT'], inc=(c == 3))
                            op('dve', lambda e: e.tensor_copy(out=dstT[:, :, tok(s)], in_=psT[:, 0:4, 0:TS]), r=['psT'], w=[dkey])

                        if NS == 4:
                            mm(0); mm(1); chain(0); mm(2); chain(1); tr(0); mm(3); chain(2); tr(1); chain(3); tr(2); tr(3)
                        else:
                            mm(0); chain(0); tr(0)

                    wkeys_cur = []

                    def compute_cq(i):
                        wkeys_cur[:] = [('W', i)]
                        lat_block('q', wslot3(i, 16, 512), qg_bc, 'qg', cqT, 'cqT', None)
                    add_block(lambda i: [(W[i][:, :], w_in_t[l, WIN_IDX[C_CQ]])], compute_cq, ('win', C_CQ))

                    def ckv_out(s, t):
                        dst = o_ckv_p[l, base + s * 128:base + s * 128 + TS, :] if P else o_ckv_s[l]
                        dma('sp', dst, t[0:TS, :], ('tf', s % 2), r=[('tf', s % 2)])

                    def kpe_mm():
                        bk = gbank()
                        for s in range(NS):
                            for kc in range(16):
                                last = kc == 15
                                op('pe', lambda e, s=s, kc=kc, last=last: e.matmul(ps[bk][0:TS, s * 64:(s + 1) * 64], lhsT=xn_lhs(kc, s), rhs=Wk[:, kc, :],
                                                                                  start=(kc == 0), stop=last),
                                   r=['Wk', 'xnT'], w=[('ps', bk)], inc=(last and s == NS - 1))
                        return bk

                    def kpe_chain(bk):
                        pk = ps[bk][0:TS, 0:NS * 64].rearrange("p (s j) -> p s j", j=64)
                        for s in range(NS):
                            op('act', lambda e, s=s: e.activation(out=sq[0:TS, 0:64], in_=pk[:, s, :], func=AF.Square, accum_out=sm[0:TS, 20 + s:21 + s]),
                               r=[('ps', bk)], w=['sq', 'smp'])
                        rstd_small(20, 40, TS, NS, 1.0 / 64, 'smp')
                        kn = rss[1][0:TS, 0:NS * 64].rearrange("p (s j) -> p s j", j=64)
                        ko = kst1[0:TS, 0:NS, :]
                        tt = rss[0][0:TS, 0:NS * 64].rearrange("p (s j) -> p s j", j=64)
                        for s in range(NS):
                            op('dve', lambda e, s=s: e.scalar_tensor_tensor(out=kn[:, s, :], in0=pk[:, s, :], scalar=sm[0:TS, 40 + s:41 + s], in1=krg_bc[0:TS, l, :],
                                                                           op0=ALU.mult, op1=ALU.mult), r=[('ps', bk), 'smp', 'krg'], w=['rs1'])
                        C_ = ropTM[0:TS, 0:NS, 0:32]
                        S_ = ropTM[0:TS, 0:NS, 32:64]
                        op('dve', lambda e: e.tensor_tensor(out=tt[:, :, 0:32], in0=kn[:, :, 0:32], in1=C_, op=ALU.mult), r=['rs1', 'ropTM'], w=['rs'])
                        op('dve', lambda e: e.tensor_tensor(out=tt[:, :, 32:64], in0=kn[:, :, 32:64], in1=S_, op=ALU.mult), r=['rs1', 'ropTM'], w=['rs'])
                        op('dve', lambda e: e.tensor_tensor(out=ko[:, :, 0:32], in0=tt[:, :, 0:32], in1=tt[:, :, 32:64], op=ALU.subtract), r=['rs'], w=['kst1'])
                        op('dve', lambda e: e.tensor_tensor(out=tt[:, :, 0:32], in0=kn[:, :, 32:64], in1=C_, op=ALU.mult), r=['rs1', 'ropTM', 'kst1'], w=['rs'])
                        op('dve', lambda e: e.tensor_tensor(out=tt[:, :, 32:64], in0=kn[:, :, 0:32], in1=S_, op=ALU.mult), r=['rs1', 'ropTM'], w=['rs'])
                        op('dve', lambda e: e.tensor_tensor(out=ko[:, :, 32:64], in0=tt[:, :, 0:32], in1=tt[:, :, 32:64], op=ALU.add), r=['rs'], w=['kst1'])
                        if P:
                            dma('sp', o_kpe_p[l, base:base + 512, :].rearrange("(s p) j -> p s j", p=128), ko, 'kst1', r=['kst1'])
                        else:
                            dma('sp', o_kpe_s[l], kst1[0:TS, 0, :], 'kst1', r=['kst1'])
                        op('act', lambda e: e.activation(out=kpb[0:TS, 0:NS, :], in_=ko, func=AF.Copy), r=['kst1'], w=['kpb'])

                    def kpe_tr():
                        for s in range(NS):
                            op('pe', lambda e, s=s: e.transpose(out=psT[0:64, s, 0:TS], in_=kpb[0:TS, s, :], identity=ident[0:TS, 0:TS]),
                               r=['kpb', 'ident'], w=['psT'], inc=(s == NS - 1))
                        if P:
                            op('dve', lambda e: e.tensor_copy(out=kpT[0:64, l, base:base + 512].rearrange("p (s t) -> p s t", t=128),
                                                              in_=psT[0:64, 0:4, :]), r=['psT'], w=['kpT'])
                        else:
                            op('dve', lambda e: e.tensor_copy(out=kpT[0:64, l, 2048:2064], in_=psT[0:64, 0, 0:16]), r=['psT'], w=['kpT'])

                    def loads_ckv(i):
                        return [(W[i][:, :], w_in_t[l, WIN_IDX[C_CKV]]), (Wk[:, :, :], w_kpe[l].rearrange("(kc p) n -> p kc n", p=128), 'Wk')]

                    def compute_ckv(i):
                        wkeys_cur[:] = [('W', i)]
                        kb = kpe_mm()
                        kpe_chain(kb)
                        lat_block('kv', wslot3(i, 16, 512), kvg_bc, 'kvg', ckT, 'ckT', ckv_out)
                        kpe_tr()
                    add_block(loads_ckv, compute_ckv, ('win', C_CKV))

                    for b in range(4):
                        def ev_bz(j, bk, b=b):
                            n = b * 4 + j
                            op('act', lambda e: e.activation(out=yb[:, n, 0:T], in_=ps[bk][:, 0:T], func=AF.Silu), r=[('ps', bk)], w=[('yb', n)])
                        fm_win_block(l, C_BZ + b * 512, ev_bz)

                    def attn_group(g):
                        def compute(i):
                            wq = W[i][:, :].rearrange("p (k h c) -> p k h c", k=4, h=4)
                            store = P and (ti < 3)
                            ob, db = 3, 4
                            kv_expand(i, g, None, T, NS, TS, ti, gi, store)

                            def kvbufs(hl):
                                if P:
                                    j = hl % 2
                                    Kp_ = KVb[:, j * 3072:j * 3072 + 1536]
                                    Vp_ = KVb[:, j * 3072 + 1536:(j + 1) * 3072].rearrange("p (s d) -> p s d", d=128)
                                    return Kp_, Vp_, [('KVk', j)], [('KVv', j)], ('Kp', j), ('Vp', j)
                                if hl % 2 == 0:
                                    Kp_ = KVb[:, 0:2048]
                                    Vp_ = KVb[:, 2048:4096].rearrange("p (s d) -> p s d", d=128)
                                    return Kp_, Vp_, [('KVk', 0), ('KVv', 0)], [('KVv', 0), ('KVk', 1)], ('Kp', 0), ('Vp', 0)
                                Kp_ = hv[:, 0:2048]
                                Vp_ = yac[:, 0:4, :].rearrange("p c (s d) -> p (c s) d", d=128)
                                return Kp_, Vp_, ['hv'], [('yac', c) for c in range(4)], ('Kp', 1), ('Vp', 1)

                            def prefetch(hl):
                                h = g * 4 + hl
                                if nprev > 0:
                                    Kp_, Vp_, kk, vk, ks_, vs_ = kvbufs(hl)
                                    dma('sp', Kp_[:, 0:nprev * 512], Ksc[gi, l, h, :, 0:nprev * 512], ks_, r=[('Ksc', gi, l, h)], w=kk)
                                    dma('sp', Vp_[:, 0:nprev * 4, :], Vsc[gi, l, g, 0:nprev * 4, :, hl * 128:(hl + 1) * 128].rearrange("s p d -> p s d"), vs_,
                                        r=[('Vsc', gi, l, g)], w=vk)

                            def chain(hl):
                                h = g * 4 + hl
                                qn = qns[h % 2]; qpe = qpes[h % 2]
                                qnk = 'qn%d' % (h % 2); qpk = 'qpe%d' % (h % 2)
                                bk = gbank()
                                fm_group(bk, lambda kc: wq[:, kc, hl, 0:128], lambda kc: cqT[:, kc, 0:T], 4, 128, T, [('W', i), 'cqT'])
                                bA = gbank()
                                fm_group(bA, lambda kc: wq[:, kc, hl, 128:192], lambda kc: cqT[:, kc, 0:T], 4, 64, T, [('W', i), 'cqT'])
                                fm_norm(bk, 128, T, mean128[:, :], qngc[:, l:l + 1], qn[:, 0:T], qnk)
                                bB = gbank()
                                fm_group(bB, lambda kc: wq[:, kc, hl, 192:256], lambda kc: cqT[:, kc, 0:T], 4, 64, T, [('W', i), 'cqT'])
                                rpi = fm_norm(bA, 64, T, mean64[0:64, :], None, None, None)
                                op('dve', lambda e: e.tensor_tensor(out=tf[0][0:64, 0:T], in0=ps[bA][0:64, 0:T], in1=gCS[0:64, 0, 0:T], op=ALU.mult),
                                   r=[('ps', bA), 'gCS'], w=[('tf', 0)])
                                op('dve', lambda e: e.tensor_tensor(out=tf[1][0:64, 0:T], in0=ps[bB][0:64, 0:T], in1=gCS[0:64, 1, 0:T], op=ALU.mult),
                                   r=[('ps', bB), 'gCS'], w=[('tf', 1)])
                                op('dve', lambda e: e.tensor_tensor(out=tf[0][0:64, 0:T], in0=tf[0][0:64, 0:T], in1=tf[1][0:64, 0:T], op=ALU.add),
                                   r=[('tf', 0), ('tf', 1)], w=[('tf', 0)])
                                op('dve', lambda e: e.tensor_tensor(out=qpe[0:64, 0:T], in0=tf[0][0:64, 0:T], in1=rss[rpi][0:64, 0:T], op=ALU.mult),
                                   r=[('tf', 0), rsk[rpi]], w=[qpk])
                                kci = k_head(i, g, hl, T, ti, gi, store)
                                return kci

                            def loop(hl, kci):
                                h = g * 4 + hl
                                qn = qns[h % 2]; qpe = qpes[h % 2]
                                qnk = 'qn%d' % (h % 2); qpk = 'qpe%d' % (h % 2)
                                kcur_ = kcurs[kci]
                                Kp_, Vp_, kk, vk, _, _ = kvbufs(hl)
                                kts = []
                                for j in range(nprev * 4):
                                    kts.append((Kp_[:, j * 128:(j + 1) * 128], kpT[0:64, l, j * 128:(j + 1) * 128], Vp_[:, j, :], 128, 0, False, kk + ['kpT'], vk))
                                for s in range(NS):
                                    kts.append((kcur_[:, tok(s)], kpT[0:64, l, base + s * 128:base + s * 128 + TS], Vg[0:TS, s, hl * 128:(hl + 1) * 128], TS,
                                                (s * 128 if P else 0), P, [kck[kci], 'kpT'], ['Vg']))
                                nk_ = len(kts)

                                def pv(ix):
                                    kt = kts[ix]
                                    pt = PT[ix % NPT]
                                    nk, c0 = kt[3], kt[4]
                                    op('pe', lambda e: e.matmul(ps[ob][:, c0:T], lhsT=kt[2], rhs=pt[0:nk, c0:T], start=(ix == 0), stop=(ix == nk_ - 1)),
                                       r=[('PT', ix % NPT)] + kt[7], w=[('ps', ob)], inc=False)
                                    op('pe', lambda e: e.matmul(ps[db][:, c0:T], lhsT=ones[0:nk, :], rhs=pt[0:nk, c0:T], start=(ix == 0), stop=(ix == nk_ - 1)),
                                       r=[('PT', ix % NPT), 'ones'], w=[('ps', db)], inc=True)

                                if not P:
                                    bk = gbank()
                                    pti = nxt('pts', NPT)
                                    pt = PT[pti]
                                    for ix, kt in enumerate(kts):
                                        nk = kt[3]
                                        cs = slice(ix * 16, ix * 16 + 16)
                                        op('pe', lambda e: e.matmul(ps[bk][0:nk, cs], lhsT=kt[0], rhs=qn[:, 0:16], start=True, stop=False),
                                           r=kt[6] + [qnk], w=[('ps', bk)], inc=False)
                                        op('pe', lambda e: e.matmul(ps[bk][0:nk, cs], lhsT=kt[1], rhs=qpe[0:64, 0:16], start=False, stop=True),
                                           r=kt[6] + [qpk], w=[('ps', bk)], inc=(ix == nk_ - 1))
                                    npv = nk_ - 1
                                    op('act', lambda e: e.activation(out=pt[:, 0:npv * 16], in_=ps[bk][:, 0:npv * 16], func=AF.Exp, scale=SCALE),
                                       r=[('ps', bk)], w=[('PT', pti)])
                                    op('act', lambda e: e.activation(out=pt[0:16, npv * 16:nk_ * 16], in_=ps[bk][0:16, npv * 16:nk_ * 16], func=AF.Exp, scale=SCALE),
                                       r=[('ps', bk)], w=[('PT', pti)])
                                    for ix, kt in enumerate(kts):
                                        nk = kt[3]
                                        cs = slice(ix * 16, ix * 16 + 16)
                                        op('pe', lambda e: e.matmul(ps[ob][:, 0:16], lhsT=kt[2], rhs=pt[0:nk, cs], start=(ix == 0), stop=(ix == nk_ - 1)),
                                           r=[('PT', pti)] + kt[7], w=[('ps', ob)], inc=False)
                                        op('pe', lambda e: e.matmul(ps[db][:, 0:16], lhsT=ones[0:nk, :], rhs=pt[0:nk, cs], start=(ix == 0), stop=(ix == nk_ - 1)),
                                           r=[('PT', pti), 'ones'], w=[('ps', db)], inc=(ix == nk_ - 1))
                                else:
                                    for ix, kt in enumerate(kts):
                                        nk, c0, diag = kt[3], kt[4], kt[5]
                                        bk = gbank()
                                        pt = PT[ix % NPT]
                                        op('pe', lambda e: e.matmul(ps[bk][0:nk, c0:T], lhsT=kt[0], rhs=qn[:, c0:T], start=True, stop=False),
                                           r=kt[6] + [qnk], w=[('ps', bk)], inc=False)
                                        op('pe', lambda e: e.matmul(ps[bk][0:nk, c0:T], lhsT=kt[1], rhs=qpe[0:64, c0:T], start=False, stop=True),
                                           r=kt[6] + [qpk], w=[('ps', bk)], inc=True)
                                        if diag:
                                            op('act', lambda e: e.activation(out=pt[0:64, c0:T], in_=ps[bk][0:64, c0:T], func=AF.Exp, scale=SCALE),
                                               r=[('ps', bk)], w=[('PT', ix % NPT)])
                                            op('act', lambda e: e.activation(out=pt[64:128, c0 + 64:T], in_=ps[bk][64:128, c0 + 64:T], func=AF.Exp, scale=SCALE),
                                               r=[('ps', bk)], w=[('PT', ix % NPT)])
                                            op('dve', lambda e: e.memset(pt[64:128, c0:c0 + 64], 0.0), w=[('PT', ix % NPT)])
                                        else:
                                            op('act', lambda e: e.activation(out=pt[0:nk, c0:T], in_=ps[bk][0:nk, c0:T], func=AF.Exp, scale=SCALE),
                                               r=[('ps', bk)], w=[('PT', ix % NPT)])
                                        if ix >= LA:
                                            pv(ix - LA)
                                    for ix in range(max(0, nk_ - LA), nk_):
                                        pv(ix)
                                rfi = nxt('rs', 2)
                                op('act', lambda e: e.activation(out=rss[rfi][:, 0:T], in_=ps[db][:, 0:T], func=AF.Ln), r=[('ps', db)], w=[rsk[rfi]])
                                op('act', lambda e: e.activation(out=rss[rfi][:, 0:T], in_=rss[rfi][:, 0:T], func=AF.Exp, scale=-1.0), r=[rsk[rfi]], w=[rsk[rfi]])
                                op('dve', lambda e: e.tensor_tensor(out=tf[0][:, 0:T], in0=ps[ob][:, 0:T], in1=rss[rfi][:, 0:T], op=ALU.mult),
                                   r=[('ps', ob), rsk[rfi]], w=[('tf', 0)])
                                op('dve', lambda e: e.tensor_tensor(out=yb[:, h, 0:T], in0=yb[:, h, 0:T], in1=tf[0][:, 0:T], op=ALU.mult),
                                   r=[('tf', 0), ('yb', h)], w=[('yb', h)])

                            prefetch(0)
                            kci_next = chain(0)
                            for hl in range(4):
                                kci = kci_next
                                if hl < 3:
                                    prefetch(hl + 1)
                                    kci_next = chain(hl + 1)
                                loop(hl, kci)
                        return compute

                    def b_front():
                        pass
                    for g in range(4):
                        add_block(qkv_loads(g), attn_group(g), ('qkv', g))
                    merge_branch(1, w_pb, 16, lambda kc: yb[:, kc, 0:T], [('yb', n) for n in range(16)])

                def out_proj():
                    for dmb in range(4):
                        def loads(i, dmb=dmb):
                            return [(W[i][:, :], w_out[l, dmb])]

                        def compute(i, dmb=dmb):
                            wv = wslot3(i, 16, 512)
                            for s in range(NS):
                                bk = gbank()
                                fm_group(bk, lambda kc, s=s: hT(kc)[:, tok(s)], lambda kc: wv[:, kc, :], 16, TS, 512, [('W', i), 'big'])
                                op('dve', lambda e, s=s: e.tensor_tensor(out=X[0:TS, s, dmb * 512:(dmb + 1) * 512], in0=ps[bk][0:TS, :],
                                                                        in1=X[0:TS, s, dmb * 512:(dmb + 1) * 512], op=ALU.add),
                                   r=[('ps', bk), ('X', s)], w=[('X', s)])
                                if P and not last_layer:
                                    op('act', lambda e, s=s: e.activation(out=sq[0:TS, :], in_=X[0:TS, s, dmb * 512:(dmb + 1) * 512], func=AF.Square,
                                                                          accum_out=sm[0:TS, 24 + s * 4 + dmb:25 + s * 4 + dmb]),
                                       r=[('X', s)], w=['sq', ('smn', s)])
                                if last_layer and dmb == 3:
                                    dst = yp[base + s * 128:base + s * 128 + TS, :] if P else ys[:, :]
                                    dma('sp', dst, X[0:TS, s, :], ('X', s), r=[('X', s)])
                        add_block(loads, compute, ('out', dmb))

                def front(i):
                    if not P and l == 0:
                        op('dve', lambda e: e.memset(X[:, 3, 0:1], 0.0), w=[('X', 3), ('X3', 0), ('X3', 1)])
                        op('dve', lambda e: e.memset(X[:, 1, 0:1], 0.0), w=[('X', 1), ('X', 2), ('W', 2)])
                    dma('sp', avg_bc[:], a_v_gain[l].partition_broadcast(128), 'avg', w=['avg'])
                    dma('sp', qg_bc[:], b_q_gain[l].partition_broadcast(128), 'qg', w=['qg'])
                    dma('sp', kvg_bc[:], b_kv_gain[l].partition_broadcast(128), 'kvg', w=['kvg'])
                    dma('sp', bias32, a_bias[l], ('tf', 1), w=[('tf', 1)])
                    op('dve', lambda e: e.tensor_copy(out=biash[:], in_=bias32), r=[('tf', 1)], w=['biash'])
                    op('dve', lambda e: e.tensor_tensor(out=bias32, in0=bias32, in1=biash[:], op=ALU.subtract), r=['biash', ('tf', 1)], w=[('tf', 1)])
                    op('dve', lambda e: e.tensor_copy(out=biasl[:], in_=bias32), r=[('tf', 1)], w=['biasl'])
                    dma('sp', gCS[0:64, :, 0:T], rope_fm[:, :, base:base + T].rearrange("c j t -> j c t"), 'gCS', w=['gCS'])
                    if l == 0:
                        if P:
                            dma('sp', ropTM[:, :, :], rope_tm[base:base + 512, :].rearrange("(i p) j -> p i j", p=128), 'ropTM', w=['ropTM'])
                        else:
                            dma('sp', ropTM[0:16, 0, :], rope_tm[2048:2064, :], 'ropTM', w=['ropTM'])
                    for c in range(2):
                        op('dve', lambda e, c=c: e.tensor_scalar(out=gCS[0:64, c, 0:T], in0=gCS[0:64, c, 0:T], scalar1=qr2c[0:64, l, c:c + 1], scalar2=1.0,
                                                                 op0=ALU.mult, op1=ALU.mult),
                           r=['gCS', 'qr2c'], w=['gCS'])
                    phase_norm()

                nb0 = len(blocks)
                items = []
                if not P:
                    items = cexp_items()
                branch_c()
                cf = blocks[nb0]['compute']
                blocks[nb0]['compute'] = (lambda i, cf=cf: (front(i), cf(i)))
                branch_a()
                if items:
                    nbA = len(blocks)
                    per = 1
                    pos = [0]

                    def wrap(cf, last):
                        def f(i):
                            cf(i)
                            n = len(items) - pos[0] if last else per
                            for _ in range(n):
                                if pos[0] < len(items):
                                    items[pos[0]]()
                                    pos[0] += 1
                        return f
                    for bj in range(nb0, nbA):
                        blocks[bj]['compute'] = wrap(blocks[bj]['compute'], bj == nbA - 1)
                branch_b()
                out_proj()

            passes = []
            for ti in range(nptiles):
                for l in range(depth):
                    passes.append(('P', ti, l))
            if sample:
                for l in range(depth):
                    passes.append(('S', 0, l))

            def load_x(grp, ti):
                if grp == 'P':
                    for s in range(4):
                        dma('sp', X[:, s, :], xp[ti * 512 + s * 128:ti * 512 + (s + 1) * 128, :], ('X', s), w=[('X', s)])
                else:
                    dma('sp', X[0:16, 0, :], xs[:, :], ('X', 0), w=[('X', 0)])

            seen_l = set()
            for (grp, ti, l) in passes:
                nb0 = len(blocks)
                cur['l'] = l
                cur['first'] = l not in seen_l
                seen_l.add(l)
                layer_pass(grp, ti, l, l == depth - 1)
                if l == 0:
                    cf = blocks[nb0]['compute']
                    blocks[nb0]['compute'] = (lambda i, cf=cf, grp=grp, ti=ti: (load_x(grp, ti), cf(i)))

            def issue_loads(bi):
                i = bi % 2
                blk = blocks[bi]
                l_ = blk['l']
                idx_map = wsc_idx[l_]
                name = blk['name']
                cached = name in idx_map
                if not cached:
                    idx_map[name] = len(idx_map)
                idx = idx_map[name]
                for ld in blk['loads'](i):
                    if len(ld) == 3:
                        dma('pool', ld[0], ld[1], 'Wk', w=['Wk'])
                    elif cached:
                        dma('sp', W[i][:, :], Wsc[l_, idx], ('W', i), r=[('Wsc', l_, idx)], w=[('W', i)])
                    else:
                        dma('pool', ld[0], ld[1], ('W', i), w=[('W', i)])
                        if USE_WSC:
                            dma('sp', Wsc[l_, idx], W[i][:, :], ('Wwb', i), r=[('W', i)], w=[('Wsc', l_, idx)])
                if not cached and not USE_WSC:
                    del idx_map[name]

            issue_loads(0)
            for bi in range(len(blocks)):
                if bi + 1 < len(blocks):
                    issue_loads(bi + 1)
                blocks[bi]['compute'](bi % 2)

            S.finish('sp')
    return nc


_NC_CACHE = {}


def _rope_tables():
    half = 32
    inv = (np.float32(10000.0) ** (-np.arange(half, dtype=np.float32) / np.float32(half))).astype(np.float32)
    pos = np.arange(NPOS, dtype=np.float32)
    ang = (pos[:, None] * inv[None, :]).astype(np.float32)
    c = np.cos(ang).astype(np.float32)
    s = np.sin(ang).astype(np.float32)
    tm = np.concatenate([c, s], axis=1).astype(np.float32)
    fm = np.zeros((2, 64, NPOS), np.float32)
    fm[0, 0:32] = c.T
    fm[0, 32:64] = c.T
    fm[1, 0:32] = -s.T
    fm[1, 32:64] = s.T
    return tm, fm


def kernel(x_prompt, x_sample, cache_ckv, cache_kpe, state_conv,
           norm_gain, w_in, a_v_gain, a_ws, a_bias,
           b_q_gain, b_w_qb, b_kv_gain, b_kr_gain, b_w_kvb,
           b_qn_gain, b_qr_gain, b_kn_gain, c_conv_w,
           w_pa, w_pb, w_pc, w_out, _cfg=None):
    cfg = _cfg or dict(nptiles=4, depth=2, sample=True)
    f = lambda a: np.ascontiguousarray(np.asarray(a, dtype=np.float32))
    qb = f(b_w_qb).reshape(2, 512, 16, 192)
    kvb = f(b_w_kvb).reshape(2, 512, 16, 256)
    w_qkv = np.concatenate([qb[..., 0:128], qb[..., 128:192], qb[..., 160:192], qb[..., 128:160],
                            kvb[..., 0:128], kvb[..., 128:256]], axis=-1).reshape(2, 512, 8192)
    qr = f(b_qr_gain)
    qr2 = np.stack([qr, np.concatenate([qr[:, 32:64], qr[:, 0:32]], axis=1)], axis=1)
    tm, fm = _rope_tables()
    key = (cfg['nptiles'], cfg['depth'], cfg['sample'])
    if key not in _NC_CACHE:
        _NC_CACHE[key] = build(**cfg)
    nc = _NC_CACHE[key]
    def tile_w(w, c0, ncols):
        w = w[:, :, c0:c0 + ncols]
        K = w.shape[1]
        return np.ascontiguousarray(w.reshape(2, K // 128, 128, ncols).transpose(0, 2, 1, 3)).reshape(2, 128, (K // 128) * ncols)
    win = f(w_in)
    w_in_t = np.stack([tile_w(win, c0, 512) for c0 in WIN_BLOCKS], axis=1)
    w_kpe = np.ascontiguousarray(win[:, :, C_KPE:C_KPE + 64])
    tl4 = lambda w, nc_: np.stack([tile_w(w, j * nc_, nc_) for j in range(4)], axis=1)
    shared = dict(norm_gain=f(norm_gain), w_in_t=w_in_t, w_kpe=w_kpe, a_v_gain=f(a_v_gain), a_ws=f(a_ws), a_bias=f(a_bias).reshape(2, 1, 512),
                  b_q_gain=f(b_q_gain), w_qkv_t=tl4(np.ascontiguousarray(w_qkv), 2048), b_kv_gain=f(b_kv_gain), b_kr_gain=f(b_kr_gain),
                  b_qn_gain=f(b_qn_gain), qr2=np.ascontiguousarray(qr2), b_kn_gain=f(b_kn_gain), c_conv_w=f(c_conv_w),
                  w_pa_t=tl4(f(w_pa), 512), w_pb_t=tl4(f(w_pb), 512), w_pc_t=tl4(f(w_pc), 512), w_out_t=tl4(f(w_out), 512),
                  rope_tm=tm, rope_fm=fm)
    xp_ = f(x_prompt); xs_ = f(x_sample); cc = f(cache_ckv); ck = f(cache_kpe); sc = f(state_conv)
    in_maps = []
    for b in range(8):
        m = dict(shared)
        m.update(xp=xp_[b], xs=xs_[b], cckv=np.ascontiguousarray(cc[:, b]), ckpe=np.ascontiguousarray(ck[:, b]),
                 sconv=np.ascontiguousarray(sc[:, b]))
        in_maps.append(m)
    res = run_bass_kernel_spmd(nc, in_maps, core_ids=list(range(8)))
    R = res.results
    st = lambda k, ax: np.stack([np.asarray(R[b][k], dtype=np.float32) for b in range(8)], axis=ax)
    return (st('yp', 0), st('ys', 0), st('o_ckv_p', 1), st('o_kpe_p', 1), st('o_conv_p', 1),
            st('o_ckv_s', 1), st('o_kpe_s', 1), st('o_conv_s', 1), st('o_av_s', 1))
```
